# Optimizing a Trainium2 kernel written in Bass

```python
import jax, jax.numpy as jnp
from jax import lax
import numpy as np

D_MODEL = 2048
BATCH = 4
SEQ = 2048
DEPTH = 2

CHUNK = 64
N_MEM = 256
SB_HEAD_DIM = 128
SB_WIDTH = D_MODEL // 2
SB_HEADS = SB_WIDTH // SB_HEAD_DIM
SB_QBLOCK = 128
GM_GROUP_DIM = 128
GM_WIDTH = D_MODEL // 2
GM_GROUPS = GM_WIDTH // GM_GROUP_DIM
GM_BLOCK = 128
XA_HEADS = 4
XA_WIDTH = D_MODEL // 2
XA_HEAD_DIM = XA_WIDTH // XA_HEADS
N_BRANCH = 3
IN_WIDTH = 3 * SB_WIDTH + 2 * GM_WIDTH + XA_WIDTH
D_FF = 5632
CONV_W = 3
EPS = 1e-6

kernel_name = "hybrid_stickbreak_gmlp_memxattn_convffn"


def rmsnorm(x, g):
    xf = x.astype(jnp.float32)
    xf = xf * lax.rsqrt(jnp.mean(xf * xf, axis=-1, keepdims=True) + EPS)
    return xf.astype(x.dtype) * g


def stick_breaking_attention(q, k, v):
    B, S, H, Dh = q.shape
    scale = Dh ** -0.5
    outs = []
    for i in range(S // SB_QBLOCK):
        q0 = i * SB_QBLOCK
        q1 = q0 + SB_QBLOCK
        qb = q[:, q0:q1]
        kb = k[:, :q1]
        vb = v[:, :q1]
        z = jnp.einsum('bthd,bshd->bhts', qb, kb).astype(jnp.float32) * scale
        t_pos = q0 + jnp.arange(SB_QBLOCK)[:, None]
        s_pos = jnp.arange(q1)[None, :]
        strict = s_pos < t_pos
        log_keep = jnp.where(strict, jax.nn.log_sigmoid(-z), 0.0)
        suffix = lax.cumsum(log_keep, axis=3, reverse=True) - log_keep
        log_a = jax.nn.log_sigmoid(z) + suffix
        a = jnp.where(strict, jnp.exp(log_a), 0.0)
        outs.append(jnp.einsum('bhts,bshd->bthd', a.astype(v.dtype), vb))
    return jnp.concatenate(outs, axis=1)


def spatial_gating(u, v, g_vnorm, w_s, b_s):
    B, S, _ = u.shape
    u = jax.nn.gelu(u)
    v = rmsnorm(jax.nn.gelu(v), g_vnorm)
    pos = jnp.arange(GM_BLOCK)
    mask = (pos[None, :] // CHUNK) <= (pos[:, None] // CHUNK)
    w = jnp.where(mask[None], w_s, jnp.zeros_like(w_s))
    vb = v.reshape(B, S // GM_BLOCK, GM_BLOCK, GM_GROUPS, GM_GROUP_DIM)
    mixed = jnp.einsum('gts,bcsgd->bctgd', w, vb) + b_s.T[None, None, :, :, None]
    return u * mixed.reshape(B, S, GM_WIDTH)


def memory_cross_attention(q, mem_kv):
    B, M, _ = mem_kv.shape
    k, v = jnp.split(mem_kv, 2, axis=-1)
    k = k.reshape(B, M, XA_HEADS, XA_HEAD_DIM)
    v = v.reshape(B, M, XA_HEADS, XA_HEAD_DIM)
    z = jnp.einsum('bthd,bmhd->bhtm', q, k).astype(jnp.float32) * (XA_HEAD_DIM ** -0.5)
    p = jax.nn.softmax(z, axis=-1)
    return jnp.einsum('bhtm,bmhd->bthd', p.astype(v.dtype), v)


def conv_ffn(h, w_up, conv_w, conv_b, w_down):
    S = h.shape[1]
    up = h @ w_up
    gate, val = jnp.split(up, 2, axis=-1)
    gp = jnp.pad(gate, ((0, 0), (CONV_W - 1, 0), (0, 0)))
    conv = conv_b + sum(conv_w[i] * gp[:, i:i + S] for i in range(CONV_W))
    return (jax.nn.gelu(conv) * val) @ w_down


def setup_inputs(seed: int = 0) -> dict:
    key = jax.random.key(seed)
    ks = jax.random.split(key, 24)
    f32 = jnp.float32

    def nrm(k, shape, fan_in):
        return jax.random.normal(k, shape, f32) * (fan_in ** -0.5)

    def gain(k, n):
        return 1.0 + 0.05 * jax.random.normal(k, (DEPTH, n), f32)

    L = DEPTH
    return {
        "x": jax.random.normal(ks[0], (BATCH, SEQ, D_MODEL), f32),
        "mem": jax.random.normal(ks[1], (BATCH, N_MEM, D_MODEL), f32),
        "g_mix_pre": gain(ks[2], D_MODEL),
        "w_in": nrm(ks[3], (L, D_MODEL, IN_WIDTH), D_MODEL),
        "g_vnorm": gain(ks[4], GM_WIDTH),
        "w_s": nrm(ks[5], (L, GM_GROUPS, GM_BLOCK, GM_BLOCK), GM_BLOCK),
        "b_s": 1.0 + 0.01 * jax.random.normal(ks[6], (L, GM_GROUPS, GM_BLOCK), f32),
        "g_mem": gain(ks[7], D_MODEL),
        "w_mem_kv": nrm(ks[8], (L, D_MODEL, 2 * XA_WIDTH), D_MODEL),
        "w_gate": nrm(ks[9], (L, D_MODEL, N_BRANCH * D_MODEL), D_MODEL),
        "b_gate": 0.01 * jax.random.normal(ks[10], (L, N_BRANCH * D_MODEL), f32),
        "w_br_sb": nrm(ks[11], (L, SB_WIDTH, D_MODEL), SB_WIDTH),
        "w_br_gm": nrm(ks[12], (L, GM_WIDTH, D_MODEL), GM_WIDTH),
        "w_br_xa": nrm(ks[13], (L, XA_WIDTH, D_MODEL), XA_WIDTH),
        "w_out": nrm(ks[14], (L, D_MODEL, D_MODEL), D_MODEL),
        "g_mix_post": gain(ks[15], D_MODEL),
        "g_ffn_pre": gain(ks[16], D_MODEL),
        "w_up": nrm(ks[17], (L, D_MODEL, 2 * D_FF), D_MODEL),
        "conv_w": nrm(ks[18], (L, CONV_W, D_FF), CONV_W),
        "conv_b": 0.01 * jax.random.normal(ks[19], (L, D_FF), f32),
        "w_down": nrm(ks[20], (L, D_FF, D_MODEL), D_FF),
        "g_ffn_post": gain(ks[21], D_MODEL),
    }


def reference(x, mem, g_mix_pre, w_in, g_vnorm, w_s, b_s, g_mem, w_mem_kv, w_gate, b_gate,
              w_br_sb, w_br_gm, w_br_xa, w_out, g_mix_post, g_ffn_pre, w_up, conv_w, conv_b,
              w_down, g_ffn_post):
    B, S, D = x.shape
    splits = [SB_WIDTH, 2 * SB_WIDTH, 3 * SB_WIDTH,
              3 * SB_WIDTH + GM_WIDTH, 3 * SB_WIDTH + 2 * GM_WIDTH]
    for l in range(DEPTH):
        h = rmsnorm(x, g_mix_pre[l])
        proj = h @ w_in[l]
        q_sb, k_sb, v_sb, u_gm, v_gm, q_xa = jnp.split(proj, splits, axis=-1)
        hs = (B, S, SB_HEADS, SB_HEAD_DIM)
        o_sb = stick_breaking_attention(q_sb.reshape(hs), k_sb.reshape(hs),
                                        v_sb.reshape(hs)).reshape(B, S, SB_WIDTH)
        o_gm = spatial_gating(u_gm, v_gm, g_vnorm[l], w_s[l], b_s[l])
        mem_kv = rmsnorm(mem, g_mem[l]) @ w_mem_kv[l]
        o_xa = memory_cross_attention(q_xa.reshape(B, S, XA_HEADS, XA_HEAD_DIM),
                                      mem_kv).reshape(B, S, XA_WIDTH)
        gates = jax.nn.sigmoid(h @ w_gate[l] + b_gate[l]).reshape(B, S, N_BRANCH, D)
        merged = (gates[:, :, 0] * (o_sb @ w_br_sb[l])
                  + gates[:, :, 1] * (o_gm @ w_br_gm[l])
                  + gates[:, :, 2] * (o_xa @ w_br_xa[l]))
        x = x + rmsnorm(merged @ w_out[l], g_mix_post[l])
        h = rmsnorm(x, g_ffn_pre[l])
        x = x + rmsnorm(conv_ffn(h, w_up[l], conv_w[l], conv_b[l], w_down[l]), g_ffn_post[l])
    return x
```

```python
import numpy as np
import concourse.bass as bass
import concourse.mybir as mybir

F32 = mybir.dt.float32
BF16 = mybir.dt.bfloat16
AF = mybir.ActivationFunctionType
ALU = mybir.AluOpType
AX = mybir.AxisListType

ENGS = ("pe", "act", "dve", "pool", "sp")


class Op:
    __slots__ = ("id", "eng", "fn", "deps", "dma_sem", "dma_n", "needs_inc", "ev")

    def __init__(self, id, eng, fn, deps, dma_sem=None, dma_n=0):
        self.id = id
        self.eng = eng
        self.fn = fn
        self.deps = deps
        self.dma_sem = dma_sem
        self.dma_n = dma_n
        self.needs_inc = False
        self.ev = None


class Prog:
    UID = 0

    def __init__(self, nc):
        self.nc = nc
        self.ops = []
        self.last_w = {}
        self.readers = {}
        self.dma_slots = {}
        self.n_dma_sems = 0

    def op(self, eng, fn, reads=(), writes=(), dma_slot=None, dma_n=1):
        strong = set()
        weak = set()
        for k in reads:
            w = self.last_w.get(k)
            if w is not None:
                strong.add(w)
        for k in writes:
            w = self.last_w.get(k)
            if w is not None:
                strong.add(w)
            for r in self.readers.get(k, ()):
                weak.add(r)
        oid = len(self.ops)
        o = Op(oid, eng, fn, None)
        dl = []
        for d in strong | weak:
            do = self.ops[d]
            if do.eng == eng and do.dma_sem is None:
                if eng == "pe" or eng == "sp":
                    continue
                if d not in strong:
                    continue
            dl.append(d)
        o.deps = dl
        if dma_slot is not None:
            if dma_slot not in self.dma_slots:
                self.dma_slots[dma_slot] = [self.n_dma_sems, 0]
                self.n_dma_sems += 1
            s = self.dma_slots[dma_slot]
            s[1] += dma_n
            o.dma_sem = s[0]
            o.dma_n = dma_n
            o.ev = ("d", s[0], 16 * s[1])
        self.ops.append(o)
        for k in reads:
            self.readers.setdefault(k, []).append(oid)
        for k in writes:
            self.last_w[k] = oid
            self.readers[k] = []
        return oid

    def emit(self, final_wait_ops=()):
        nc = self.nc
        ops = self.ops
        for o in ops:
            for d in o.deps:
                ops[d].needs_inc = True
        for d in final_wait_ops:
            ops[d].needs_inc = True
        cnt = {e: 0 for e in ENGS}
        for o in ops:
            if o.dma_sem is None and o.needs_inc:
                cnt[o.eng] += 1
                o.ev = ("e", o.eng, cnt[o.eng])
        per_eng = {e: [o for o in ops if o.eng == e] for e in ENGS}
        Prog.UID += 1
        uid = Prog.UID
        esem = {e: nc.alloc_semaphore(name="s%d_%s" % (uid, e)) for e in ENGS}
        dsem = [nc.alloc_semaphore(name="d%d_%d" % (uid, i)) for i in range(self.n_dma_sems)]
        allsems = list(esem.values()) + dsem
        with nc.Block() as block:

            def semof(ev):
                return esem[ev[1]] if ev[0] == "e" else dsem[ev[1]]

            def run(engname, eng):
                seen = {}
                for o in per_eng[engname]:
                    need = {}
                    for d in o.deps:
                        ev = ops[d].ev
                        key = (ev[0], ev[1])
                        if seen.get(key, 0) >= ev[2]:
                            continue
                        if need.get(key, 0) < ev[2]:
                            need[key] = ev[2]
                    for key, v in need.items():
                        eng.wait_ge(semof((key[0], key[1], v)), v)
                        seen[key] = v
                    r = o.fn(eng)
                    if o.dma_sem is not None:
                        rs = r if isinstance(r, (list, tuple)) else [r]
                        assert len(rs) == o.dma_n, (len(rs), o.dma_n)
                        for ri in rs:
                            ri.then_inc(dsem[o.dma_sem], 16)
                    elif o.needs_inc:
                        r.then_inc(esem[engname], 1)
                if engname == "sp":
                    for d in final_wait_ops:
                        ev = ops[d].ev
                        eng.wait_ge(semof(ev), ev[2])

            @block.tensor
            def _(e):
                run("pe", e)

            @block.scalar
            def _(e):
                run("act", e)

            @block.vector
            def _(e):
                run("dve", e)

            @block.gpsimd
            def _(e):
                run("pool", e)

            @block.sync
            def _(e):
                run("sp", e)
        nc.clear_and_free_semaphores(allsems)
        nc.all_engine_barrier()
import contextlib
import numpy as np
import concourse.bass as bass
import concourse.mybir as mybir

D = 2048
T = 1024
KC = 16
EPS = 1e-6
NEG = -30000.0
SQ128 = 11.3125


def dram(nc, name, shape, dt, kind):
    return nc.dram_tensor(name, list(shape), dt, kind=kind).ap()


class Ctx:
    def __init__(self, nc):
        self.nc = nc
        self.st = contextlib.ExitStack()
        self.p = Prog(nc)
        self.n = 0
        self.finals = []

    UID = [0]

    def sb(self, shape, dt, name=None):
        Ctx.UID[0] += 1
        return self.st.enter_context(self.nc.sbuf_tensor("%s_%d" % (name or "sb", Ctx.UID[0]), list(shape), dt))

    def ps(self, nbanks, name=None):
        Ctx.UID[0] += 1
        return self.st.enter_context(self.nc.psum_tensor("%s_%d" % (name or "ps", Ctx.UID[0]), [128, nbanks, 512], F32))

    def finish(self):
        self.p.emit(final_wait_ops=self.finals)
        self.st.close()


class TP:
    def __init__(self, p, tag):
        self.p = p
        self.tag = tag

    def _k(self, k):
        if isinstance(k, tuple) and len(k) and isinstance(k[0], str) and k[0].startswith("@"):
            return k
        return (self.tag, k)

    def op(self, eng, fn, reads=(), writes=(), dma_slot=None, dma_n=1):
        return self.p.op(eng, fn, reads=[self._k(k) for k in reads], writes=[self._k(k) for k in writes],
                         dma_slot=(None if dma_slot is None else self._k(dma_slot)), dma_n=dma_n)


class WStream:
    def __init__(self, c, name, kc, cw, nbuf, p=None):
        self.c = c
        self.p = p or c.p
        self.name = name
        self.kc = kc
        self.nbuf = nbuf
        self.bufs = [c.sb([128, kc, cw], BF16, name="%s_%d" % (name, i)) for i in range(nbuf)]
        self.i = 0

    def load(self, w2d, kc=None, cw=None):
        b = self.i % self.nbuf
        first = (self.i == 0)
        self.i += 1
        buf = self.bufs[b]
        kc = kc or self.kc
        ncol = w2d.shape[1]
        ncc = ncol // 128
        keys = [(self.name, b, cc) for cc in range(ncc)]
        src = w2d.rearrange("(kc p) n -> p kc n", p=128)
        if first and ncc > 1:
            for cc in range(ncc):
                self.p.op("pool", lambda e, cc=cc: e.dma_start(out=buf[:, 0:kc, cc * 128:(cc + 1) * 128], in_=src[:, :, cc * 128:(cc + 1) * 128]),
                            writes=[keys[cc]], dma_slot=(self.name, b, cc, "s"))
        else:
            self.p.op("pool", lambda e: e.dma_start(out=buf[:, 0:kc, 0:ncol], in_=src), writes=keys, dma_slot=(self.name, b))
        return buf, keys


def stream(items, loader, depth):
    items = list(items)
    q = []
    nxt = 0
    for i in range(len(items)):
        while nxt < len(items) and nxt < i + depth:
            q.append(loader(items[nxt]))
            nxt += 1
        yield items[i], q.pop(0)


def rms_stats(c, src, ncols, ps_bank, ones, epsb, scr, rstd, keys_r, key_w, kc=KC, inv_d=1.0 / D, p=None, pskey="pss_bank"):
    p = p or c.p
    p.op("act", lambda e: e.activation(out=scr[:, 0:kc, 0:ncols], in_=src, func=AF.Square),
         reads=keys_r, writes=[("scr", id(scr))])

    def mm(e):
        r = None
        for k in range(kc):
            r = e.matmul(ps_bank[:, 0:ncols], lhsT=ones[:], rhs=scr[:, k, 0:ncols], start=(k == 0), stop=(k == kc - 1))
        return r
    p.op("pe", mm, reads=[("scr", id(scr))], writes=[pskey])
    p.op("act", lambda e: e.activation(out=rstd, in_=ps_bank[:, 0:ncols], func=AF.Ln, bias=epsb[:], scale=inv_d),
         reads=[pskey, "epsb"], writes=[key_w])
    p.op("act", lambda e: e.activation(out=rstd, in_=rstd, func=AF.Exp, scale=-0.5), reads=[key_w], writes=[key_w])


def consts(c, p=None):
    nc = c.nc
    p = p or c.p
    ones = c.sb([128, 128], BF16, "ones")
    epsb = c.sb([128, 1], F32, "epsb")
    p.op("dve", lambda e: e.memset(ones[:], 1.0), writes=["ones"])
    p.op("dve", lambda e: e.memset(epsb[:], EPS), writes=["epsb"])
    return ones, epsb


def phase_proj(nc, xT, g_pre, w_in, gvn, hT, qT, kT, v, uT, vn, qxT, xT_alt=None, sel=None, xownT=None, groups=(0, 1)):
    c = Ctx(nc)
    p = c.p
    ones, epsb = consts(c)
    xt = c.sb([128, KC, T], F32, "xt")
    ht = c.sb([128, KC, T], BF16, "ht")
    gsb = c.sb([128, KC], F32, "gsb")
    gvb = c.sb([128, 1024], F32, "gvb")
    scr = c.sb([128, KC, 512], BF16, "scr")
    rstd = c.sb([128, 2, 512], F32, "rstd")
    gv = c.sb([128, 8, 1024], F32, "gv")
    ssv = c.sb([128, 8, 2], F32, "ssv")
    rsv = c.sb([128, 8], F32, "rsv")
    junk = c.sb([128, 512], BF16, "junk")
    stg = [c.sb([128, T], BF16, "stg%d" % i) for i in range(3)]
    stv = [c.sb([128, 1024], BF16, "stv%d" % i) for i in range(2)]
    ps = c.ps(6, "psg")
    pss = c.ps(1, "pss")
    ws = WStream(c, "w", KC, 512, 2)

    for g in range(2):
        p.op("sp", lambda e, g=g: e.dma_start(out=xt[:, :, g * 512:(g + 1) * 512],
                                             in_=xT.rearrange("(kc p) t -> p kc t", p=128)[:, :, g * 512:(g + 1) * 512]),
             writes=[("xt", g)], dma_slot=("xt", g))
    p.op("sp", lambda e: e.dma_start(out=gsb[:], in_=g_pre), writes=["gsb"], dma_slot="gsb")
    p.op("sp", lambda e: e.dma_start(out=gvb[:], in_=gvn.partition_broadcast(128)), writes=["gvb"], dma_slot="gvb")

    if xT_alt is not None:
        selsb = c.sb([128, 2], F32, "selsb")
        p.op("sp", lambda e: e.dma_start(out=selsb[:], in_=sel), writes=["selsb"], dma_slot="selsb")
        gvkeys = [("gv", tt, hf) for tt in range(8) for hf in range(2)]
        for rnd in range(2):
            p.op("sp", lambda e, rnd=rnd: e.dma_start(out=gv[:], in_=xT_alt[rnd * 1024:(rnd + 1) * 1024, :].rearrange("(kc p) t -> p kc t", p=128)),
                 writes=gvkeys, dma_slot=("alt", rnd))
            for kk in range(8):
                k = rnd * 8 + kk
                p.op("dve", lambda e, k=k: e.tensor_scalar(out=xt[:, k, :], in0=xt[:, k, :], scalar1=selsb[:, 0:1], scalar2=None, op0=ALU.mult),
                     reads=[("xt", 0), ("xt", 1), "selsb"], writes=[("xt", 0), ("xt", 1)])
                p.op("dve", lambda e, k=k, kk=kk: e.scalar_tensor_tensor(out=xt[:, k, :], in0=gv[:, kk, :], scalar=selsb[:, 1:2], in1=xt[:, k, :],
                                                                        op0=ALU.mult, op1=ALU.add),
                     reads=[("xt", 0), ("xt", 1), "selsb"] + gvkeys, writes=[("xt", 0), ("xt", 1)])
        o = p.op("sp", lambda e: e.dma_start(out=xownT.rearrange("(kc p) t -> p kc t", p=128), in_=xt[:]), reads=[("xt", 0), ("xt", 1)], dma_slot="xown")
        c.finals.append(o)
    for g in range(2):
        rms_stats(c, xt[:, :, g * 512:(g + 1) * 512], 512, pss[:, 0, :], ones, epsb, scr, rstd[:, g, :],
                  [("xt", g), "ones", "epsb"], ("rstd", g))
        for k in range(KC):
            p.op("dve", lambda e, k=k, g=g: e.scalar_tensor_tensor(
                out=ht[:, k, g * 512:(g + 1) * 512], in0=xt[:, k, g * 512:(g + 1) * 512], scalar=gsb[:, k:k + 1],
                in1=rstd[:, g, :], op0=ALU.mult, op1=ALU.mult),
                reads=[("xt", g), "gsb", ("rstd", g)], writes=[("ht", g, k)])
    htk = [("ht", g, k) for g in range(2) for k in range(KC)]
    o = p.op("sp", lambda e: e.dma_start(out=hT.rearrange("(kc p) t -> p kc t", p=128), in_=ht[:]), reads=htk, dma_slot="hTout")
    c.finals.append(o)

    nb = [0]
    nst = [0]
    nsv = [0]

    def feat_major(buf, wkey, dst, row0, func, grp=(0, 1)):
        for cc in range(4):
            sb_ = stg[nst[0] % 3]
            skey = ("stg", nst[0] % 3)
            nst[0] += 1
            for g in grp:
                b = nb[0] % 6
                nb[0] += 1

                def mm(e, b=b, cc=cc, g=g):
                    r = None
                    for k in range(KC):
                        r = e.matmul(ps[:, b, :], lhsT=buf[:, k, cc * 128:(cc + 1) * 128], rhs=ht[:, k, g * 512:(g + 1) * 512],
                                     start=(k == 0), stop=(k == KC - 1))
                    return r
                p.op("pe", mm, reads=[wkey[cc]] + [("ht", g, k) for k in range(KC)], writes=[("ps", b)])
                p.op("act", lambda e, b=b, g=g, sb_=sb_: e.activation(out=sb_[:, g * 512:(g + 1) * 512], in_=ps[:, b, :], func=func),
                     reads=[("ps", b)], writes=[skey + (g,)])
            r0 = row0 + cc * 128
            o = p.op("sp", lambda e, sb_=sb_, r0=r0: e.dma_start(out=dst[r0:r0 + 128, :], in_=sb_[:]),
                     reads=[skey + (g,) for g in grp], dma_slot=skey)
            c.finals.append(o)

    for (s, (buf, wkey)) in stream(range(12), lambda s: ws.load(w_in[:, s * 512:(s + 1) * 512]), 2):
        sec, half = divmod(s, 2)
        if sec == 0:
            feat_major(buf, wkey, qT, half * 512, AF.Copy, groups)
        elif sec == 1:
            feat_major(buf, wkey, kT, half * 512, AF.Copy)
        elif sec == 3:
            feat_major(buf, wkey, uT, half * 512, AF.Gelu_apprx_tanh, groups)
        elif sec == 5:
            feat_major(buf, wkey, qxT, half * 512, AF.Copy, groups)
        else:
            tts = range(8) if sec == 2 else [tt for tt in range(8) if (tt // 4) in groups]
            for tt in tts:
                b = nb[0] % 6
                nb[0] += 1

                def mm(e, b=b, tt=tt, buf=buf):
                    r = None
                    for k in range(KC):
                        r = e.matmul(ps[:, b, :], lhsT=ht[:, k, tt * 128:(tt + 1) * 128], rhs=buf[:, k, :],
                                     start=(k == 0), stop=(k == KC - 1))
                    return r
                p.op("pe", mm, reads=list(wkey) + [("ht", tt // 4, k) for k in range(KC)], writes=[("ps", b)])
                if sec == 2:
                    sv = stv[nsv[0] % 2]
                    svk = ("stv", nsv[0] % 2)
                    nsv[0] += 1
                    p.op("act", lambda e, b=b, sv=sv: e.activation(out=sv[:, 0:512], in_=ps[:, b, :], func=AF.Copy),
                         reads=[("ps", b)], writes=[svk])
                    o = p.op("sp", lambda e, sv=sv, tt=tt, half=half: e.dma_start(
                        out=v[tt * 128:(tt + 1) * 128, half * 512:(half + 1) * 512], in_=sv[:, 0:512]),
                        reads=[svk], dma_slot=svk)
                    c.finals.append(o)
                else:
                    p.op("act", lambda e, b=b, tt=tt, half=half: e.activation(
                        out=gv[:, tt, half * 512:(half + 1) * 512], in_=ps[:, b, :], func=AF.Gelu_apprx_tanh),
                        reads=[("ps", b)], writes=[("gv", tt, half)])
                    p.op("act", lambda e, tt=tt, half=half: e.activation(
                        out=junk[:], in_=gv[:, tt, half * 512:(half + 1) * 512], func=AF.Square,
                        accum_out=ssv[:, tt, half:half + 1]),
                        reads=[("gv", tt, half)], writes=[("ssv", tt, half), "junk"])
    p.op("dve", lambda e: e.tensor_tensor(out=rsv[:], in0=ssv[:, :, 0], in1=ssv[:, :, 1], op=ALU.add),
         reads=[("ssv", tt, h_) for tt in range(8) for h_ in range(2)], writes=["rsv"])
    p.op("act", lambda e: e.activation(out=rsv[:], in_=rsv[:], func=AF.Ln, bias=epsb[:], scale=1.0 / 1024), reads=["rsv", "epsb"], writes=["rsv"])
    p.op("act", lambda e: e.activation(out=rsv[:], in_=rsv[:], func=AF.Exp, scale=-0.5), reads=["rsv"], writes=["rsv"])
    for tt in [tt for tt in range(8) if (tt // 4) in groups]:
        sv = stv[nsv[0] % 2]
        svk = ("stv", nsv[0] % 2)
        nsv[0] += 1
        p.op("dve", lambda e, tt=tt, sv=sv: e.scalar_tensor_tensor(
            out=sv[:], in0=gv[:, tt, :], scalar=rsv[:, tt:tt + 1], in1=gvb[:], op0=ALU.mult, op1=ALU.mult),
            reads=[("gv", tt, 0), ("gv", tt, 1), "rsv", "gvb"], writes=[svk])
        o = p.op("sp", lambda e, sv=sv, tt=tt: e.dma_start(out=vn[tt * 128:(tt + 1) * 128, :], in_=sv[:]), reads=[svk], dma_slot=svk)
        c.finals.append(o)
    c.finish()


def phase_attn2(nc, qT, kT_own, kT_prev, v_own, v_prev, prevmask, osbT, has_prev=True, side=(), groups=(0, 1)):
    c = Ctx(nc)
    p = c.p
    scale = 128 ** -0.5
    ident = c.sb([128, 128], BF16, "ident")
    negU = c.sb([128, 128], BF16, "negU")
    negO = c.sb([128, 128], BF16, "negO")
    negm = c.sb([128, 4, 512], BF16, "negm")
    pmask = c.sb([128, 512], BF16, "pmask")
    one1 = c.sb([128, 1], F32, "one1")
    kt = [c.sb([128, 2 * T], BF16, "kt%d" % i) for i in range(2)]
    vh = [c.sb([128, 16, 128], BF16, "vh%d" % i) for i in range(2)]
    qh = [c.sb([128, T], BF16, "qh%d" % i) for i in range(2)]
    eb = [c.sb([128, 1024], F32, "eb%d" % i) for i in range(2)]
    spb = [c.sb([128, 1024], BF16, "spb%d" % i) for i in range(4)]
    Gb = [c.sb([128, 1024], BF16, "Gb%d" % i) for i in range(3)]
    ab = [c.sb([128, 1024], BF16, "ab%d" % i) for i in range(3)]
    ost = [c.sb([128, T], BF16, "ost%d" % i) for i in range(2)]
    ps1 = c.ps(2, "ps1")
    ps2 = c.ps(2, "ps2")
    pso = c.ps(2, "pso")

    p.op("pool", lambda e: e.memset(ident[:], 1.0), writes=["ident"])
    p.op("pool", lambda e: e.affine_select(out=ident[:], in_=ident[:], pattern=[[-1, 128]], compare_op=ALU.is_equal,
                                           fill=0.0, base=0, channel_multiplier=1), reads=["ident"], writes=["ident"])
    p.op("pool", lambda e: e.memset(negU[:], -SQ128), writes=["negU"])
    p.op("pool", lambda e: e.affine_select(out=negU[:], in_=negU[:], pattern=[[-1, 128]], compare_op=ALU.is_ge,
                                           fill=0.0, base=0, channel_multiplier=1), reads=["negU"], writes=["negU"])
    p.op("pool", lambda e: e.memset(negO[:], -SQ128), writes=["negO"])
    p.op("pool", lambda e: e.memset(negm[:], 0.0), writes=["negm"])
    for r in range(4):
        p.op("pool", lambda e, r=r: e.affine_select(out=negm[:, r, :], in_=negm[:, r, :], pattern=[[1, 512]],
                                                    compare_op=ALU.is_gt, fill=NEG, base=-128 * r, channel_multiplier=-1),
             reads=["negm"], writes=["negm"])
    p.op("pool", lambda e: e.memset(one1[:], 1.0), writes=["one1"])
    if prevmask is not None:
        p.op("sp", lambda e: e.dma_start(out=pmask[:], in_=prevmask), writes=["pmask"], dma_slot="pmask")
    kb_lo = 0 if has_prev else 8

    def load_head(h):
        i = h % 2
        def ld(e):
            r = [e.dma_start(out=kt[i][:, T:2 * T], in_=kT_own[h * 128:(h + 1) * 128, :]),
                 e.dma_start(out=vh[i][:, 8:16, :], in_=v_own[:, h * 128:(h + 1) * 128].rearrange("(b p) d -> p b d", p=128)),
                 e.dma_start(out=qh[i][:], in_=qT[h * 128:(h + 1) * 128, :])]
            if has_prev:
                r += [e.dma_start(out=kt[i][:, 0:T], in_=kT_prev[h * 128:(h + 1) * 128, :]),
                      e.dma_start(out=vh[i][:, 0:8, :], in_=v_prev[:, h * 128:(h + 1) * 128].rearrange("(b p) d -> p b d", p=128))]
            return r
        p.op("sp", ld, writes=[("hd", i)], dma_slot=("hd", i), dma_n=(5 if has_prev else 3))
        return i

    ps1f = ps1[:].rearrange("p b n -> p (b n)")
    ps2f = ps2[:].rearrange("p b n -> p (b n)")
    psof = pso[:].rearrange("p b n -> p (b n)")
    steps = []
    for h in range(8):
        hsteps = []
        if 1 in groups:
            for kb in range(15, 11, -1):
                hsteps.append(dict(h=h, kb=kb, halves=[1], hfirst=False))
        for kb in range(11, kb_lo - 1, -1):
            hsteps.append(dict(h=h, kb=kb, halves=[hf for hf in (0, 1) if hf in groups], hfirst=False))
        hsteps[0]["hfirst"] = True
        steps += hsteps
    NSP = 4
    st = dict(G=None, gi=None, valid=[False, False], nG=0)
    head_slot = {}

    def zops(t, idx):
        h, kb, halves = t["h"], t["kb"], t["halves"]
        i = head_slot[h]
        hk = ("hd", i)
        ksl = kt[i][:, kb * 128:(kb + 1) * 128]
        c0 = 512 * halves[0]
        W = 512 * len(halves)
        t.update(i=i, hk=hk, c0=c0, W=W)
        mm = {}
        for hf in halves:
            m = [(ksl, qh[i][:, hf * 512:(hf + 1) * 512])]
            r = kb - (8 + 4 * hf)
            if r >= 0:
                m.append((ident[:], negm[:, r, :]))
            if kb < 8 and prevmask is not None:
                m.append((ident[:], pmask[:]))
            mm[hf] = m
        t["mm"] = mm
        ei = idx % 2
        si = idx % NSP
        t["si"] = si

        def zmm(e, mm=mm, halves=halves):
            rr = None
            for hf in halves:
                for j, (l_, r_) in enumerate(mm[hf]):
                    rr = e.matmul(ps1[:, hf, :], lhsT=l_, rhs=r_, start=(j == 0), stop=(j == len(mm[hf]) - 1))
            return rr
        p.op("pe", zmm, reads=[hk, "ident", "negm", "pmask"], writes=[("ps1", hf) for hf in halves])
        p.op("act", lambda e: e.activation(out=eb[ei][:, c0:c0 + W], in_=ps1f[:, c0:c0 + W], func=AF.Exp, scale=scale),
             reads=[("ps1", hf) for hf in halves], writes=[("eb", ei, hf) for hf in halves])
        p.op("act", lambda e: e.activation(out=spb[si][:, c0:c0 + W], in_=eb[ei][:, c0:c0 + W], func=AF.Ln, bias=one1[:], scale=1.0),
             reads=[("eb", ei, hf) for hf in halves] + ["one1"], writes=[("spb", si, hf) for hf in halves])
        if t["hfirst"]:
            st["valid"] = [False, False]
            st["G"] = None
        t["G"] = st["G"]
        t["Gi"] = st["gi"]
        t["Gvalid"] = list(st["valid"])
        if kb > kb_lo:
            gi = st["nG"] % 3
            st["nG"] += 1
            Gn = Gb[gi]
            Gc = st["G"]
            gci = st["gi"]
            vh_ = [hf for hf in halves if st["valid"][hf]]
            nv = [hf for hf in halves if not st["valid"][hf]]
            if len(vh_) == 2:
                p.op("dve", lambda e: e.tensor_tensor(out=Gn[:], in0=Gc[:], in1=spb[si][:], op=ALU.add),
                     reads=[("Gb", gci, 0), ("Gb", gci, 1), ("spb", si, 0), ("spb", si, 1)], writes=[("Gb", gi, 0), ("Gb", gi, 1)])
            else:
                for hf in vh_:
                    sl = slice(hf * 512, (hf + 1) * 512)
                    p.op("dve", lambda e, sl=sl: e.tensor_tensor(out=Gn[:, sl], in0=Gc[:, sl], in1=spb[si][:, sl], op=ALU.add),
                         reads=[("Gb", gci, hf), ("spb", si, hf)], writes=[("Gb", gi, hf)])
            for hf in nv:
                sl = slice(hf * 512, (hf + 1) * 512)
                p.op("dve", lambda e, sl=sl: e.tensor_copy(out=Gn[:, sl], in_=spb[si][:, sl]),
                     reads=[("spb", si, hf)], writes=[("Gb", gi, hf)])
                st["valid"][hf] = True
            st["G"] = Gn
            st["gi"] = gi

    def ps2ops(t, idx):
        halves, si, c0, W = t["halves"], t["si"], t["c0"], t["W"]
        ai = idx % 3
        t["ai"] = ai
        mm = {}
        rd = [t["hk"], "ident", "negm", "pmask", "negU", "negO"]
        for hf in halves:
            sl = slice(hf * 512, (hf + 1) * 512)
            m = list(t["mm"][hf]) + [(negU[:], spb[si][:, sl])]
            rd.append(("spb", si, hf))
            if t["Gvalid"][hf]:
                m.append((negO[:], t["G"][:, sl]))
                rd.append(("Gb", t["Gi"], hf))
            mm[hf] = m

        def mm2(e, mm=mm, halves=halves):
            rr = None
            for hf in halves:
                for j, (l_, r_) in enumerate(mm[hf]):
                    rr = e.matmul(ps2[:, hf, :], lhsT=l_, rhs=r_, start=(j == 0), stop=(j == len(mm[hf]) - 1))
            return rr
        p.op("pe", mm2, reads=rd, writes=[("ps2", hf) for hf in halves])
        p.op("act", lambda e: e.activation(out=ab[ai][:, c0:c0 + W], in_=ps2f[:, c0:c0 + W], func=AF.Exp, scale=scale),
             reads=[("ps2", hf) for hf in halves], writes=[("ab", ai, hf) for hf in halves])

    def avops(t, idx):
        h, kb, i, ai, halves = t["h"], t["kb"], t["i"], t["ai"], t["halves"]

        def av(e):
            rr = None
            for hf in halves:
                rr = e.matmul(pso[:, hf, :], lhsT=vh[i][:, kb, :], rhs=ab[ai][:, hf * 512:(hf + 1) * 512],
                              start=(kb == 11 + 4 * hf), stop=(kb == kb_lo))
            return rr
        p.op("pe", av, reads=[t["hk"]] + [("ab", ai, hf) for hf in halves], writes=[("pso", hf) for hf in halves])
        if kb == kb_lo:
            oi = h % 2
            p.op("dve", lambda e: e.tensor_copy(out=ost[oi][:], in_=psof[:, 0:1024]),
                 reads=[("pso", hf) for hf in groups], writes=[("ost", oi)])
            o = p.op("sp", lambda e: e.dma_start(out=osbT[h * 128:(h + 1) * 128, :], in_=ost[oi][:]),
                     reads=[("ost", oi)], dma_slot=("ost", oi))
            c.finals.append(o)

    n = len(steps)
    head_slot[0] = load_head(0)
    head_slot[1] = load_head(1)
    gens = []
    if side:
        psx = PsRing(c.ps(2, "psx"), 2)
        gens = [mk(c, psx) for mk in side]
    nper = 1

    def side_step():
        while gens:
            try:
                next(gens[0])
                return
            except StopIteration:
                gens.pop(0)
    for step in range(n + 2):
        for _ in range(nper):
            side_step()
        if step < n:
            zops(steps[step], step)
        if 0 <= step - 1 < n:
            ps2ops(steps[step - 1], step - 1)
        if 0 <= step - 2 < n:
            t = steps[step - 2]
            avops(t, step - 2)
            if t["kb"] == kb_lo and t["h"] + 2 < 8:
                head_slot[t["h"] + 2] = load_head(t["h"] + 2)
    while gens:
        side_step()
    c.finish()


def phase_attn(nc, qT, kT_own, kT_prev, v_own, v_prev, prevmask, osbT, has_prev=True, side=()):
    c = Ctx(nc)
    p = c.p
    scale = 128 ** -0.5
    ident = c.sb([128, 128], BF16, "ident")
    negU = c.sb([128, 128], BF16, "negU")
    negO = c.sb([128, 128], BF16, "negO")
    negm = c.sb([128, 4, 512], BF16, "negm")
    pmask = c.sb([128, 512], BF16, "pmask")
    one1 = c.sb([128, 1], F32, "one1")
    kt = [c.sb([128, 2 * T], BF16, "kt%d" % i) for i in range(2)]
    vh = [c.sb([128, 16, 128], BF16, "vh%d" % i) for i in range(2)]
    qh = [c.sb([128, T], BF16, "qh%d" % i) for i in range(2)]
    eb = [c.sb([128, 512], F32, "eb%d" % i) for i in range(2)]
    spb = [c.sb([128, 512], BF16, "spb%d" % i) for i in range(4)]
    Gb = [c.sb([128, 512], BF16, "Gb%d" % i) for i in range(3)]
    ab = [c.sb([128, 512], BF16, "ab%d" % i) for i in range(3)]
    ost = [c.sb([128, T], BF16, "ost%d" % i) for i in range(2)]
    ps1 = c.ps(2, "ps1")
    ps2 = c.ps(2, "ps2")
    pso = c.ps(2, "pso")

    p.op("pool", lambda e: e.memset(ident[:], 1.0), writes=["ident"])
    p.op("pool", lambda e: e.affine_select(out=ident[:], in_=ident[:], pattern=[[-1, 128]], compare_op=ALU.is_equal,
                                           fill=0.0, base=0, channel_multiplier=1), reads=["ident"], writes=["ident"])
    p.op("pool", lambda e: e.memset(negU[:], -SQ128), writes=["negU"])
    p.op("pool", lambda e: e.affine_select(out=negU[:], in_=negU[:], pattern=[[-1, 128]], compare_op=ALU.is_ge,
                                           fill=0.0, base=0, channel_multiplier=1), reads=["negU"], writes=["negU"])
    p.op("pool", lambda e: e.memset(negO[:], -SQ128), writes=["negO"])
    p.op("pool", lambda e: e.memset(negm[:], 0.0), writes=["negm"])
    for r in range(4):
        p.op("pool", lambda e, r=r: e.affine_select(out=negm[:, r, :], in_=negm[:, r, :], pattern=[[1, 512]],
                                                    compare_op=ALU.is_gt, fill=NEG, base=-128 * r, channel_multiplier=-1),
             reads=["negm"], writes=["negm"])
    p.op("pool", lambda e: e.memset(one1[:], 1.0), writes=["one1"])
    if prevmask is not None:
        p.op("sp", lambda e: e.dma_start(out=pmask[:], in_=prevmask), writes=["pmask"], dma_slot="pmask")
    kb_lo = 0 if has_prev else 8

    def load_head(h):
        i = h % 2
        def ld(e):
            r = [e.dma_start(out=kt[i][:, T:2 * T], in_=kT_own[h * 128:(h + 1) * 128, :]),
                 e.dma_start(out=vh[i][:, 8:16, :], in_=v_own[:, h * 128:(h + 1) * 128].rearrange("(b p) d -> p b d", p=128)),
                 e.dma_start(out=qh[i][:], in_=qT[h * 128:(h + 1) * 128, :])]
            if has_prev:
                r += [e.dma_start(out=kt[i][:, 0:T], in_=kT_prev[h * 128:(h + 1) * 128, :]),
                      e.dma_start(out=vh[i][:, 0:8, :], in_=v_prev[:, h * 128:(h + 1) * 128].rearrange("(b p) d -> p b d", p=128))]
            return r
        p.op("sp", ld, writes=[("hd", i)], dma_slot=("hd", i), dma_n=(5 if has_prev else 3))
        return i

    tiles = []
    for h in range(8):
        for gq in range(2):
            top = 8 + 4 * gq + 3
            for kb in range(top, kb_lo - 1, -1):
                tiles.append(dict(h=h, gq=gq, kb=kb, top=top, first=(kb == top), last=(kb == kb_lo)))
    NSP = 4
    state = dict(loaded=-1, G=None, gkey=None, nG=0)
    head_slot = {}

    def ensure_head(h):
        assert h in head_slot, h

    def zops(t, idx):
        h, gq, kb = t["h"], t["gq"], t["kb"]
        ensure_head(h)
        i = head_slot[h]
        hk = ("hd", i)
        r = kb - (8 + 4 * gq)
        ksl = kt[i][:, kb * 128:(kb + 1) * 128]
        qsl = qh[i][:, gq * 512:(gq + 1) * 512]
        mms = [(ksl, qsl)]
        if r >= 0:
            mms.append((ident[:], negm[:, r, :]))
        if kb < 8 and prevmask is not None:
            mms.append((ident[:], pmask[:]))
        t["mms"] = mms
        t["hk"] = hk
        t["i"] = i
        b1 = idx % 2
        ei = idx % 2
        si = idx % NSP
        t["si"] = si

        def zmm(e, mms=mms, b1=b1):
            rr = None
            for j, (l_, r_) in enumerate(mms):
                rr = e.matmul(ps1[:, b1, :], lhsT=l_, rhs=r_, start=(j == 0), stop=(j == len(mms) - 1))
            return rr
        p.op("pe", zmm, reads=[hk, "ident", "negm", "pmask"], writes=[("ps1", b1)])
        p.op("act", lambda e, b1=b1, ei=ei: e.activation(out=eb[ei][:], in_=ps1[:, b1, :], func=AF.Exp, scale=scale),
             reads=[("ps1", b1)], writes=[("eb", ei)])
        p.op("act", lambda e, ei=ei, si=si: e.activation(out=spb[si][:], in_=eb[ei][:], func=AF.Ln, bias=one1[:], scale=1.0),
             reads=[("eb", ei), "one1"], writes=[("spb", si)])
        if t["first"]:
            state["G"] = None
            state["gkey"] = None
        t["G"] = state["G"]
        t["gkey"] = state["gkey"]
        if not t["last"]:
            if state["G"] is None:
                state["G"] = spb[si]
                state["gkey"] = ("spb", si)
            else:
                gi = state["nG"] % 3
                state["nG"] += 1
                Gn = Gb[gi]
                Gc = state["G"]
                p.op("dve", lambda e, Gn=Gn, Gc=Gc, si=si: e.tensor_tensor(out=Gn[:], in0=Gc[:], in1=spb[si][:], op=ALU.add),
                     reads=[state["gkey"], ("spb", si)], writes=[("Gb", gi)])
                state["G"] = Gn
                state["gkey"] = ("Gb", gi)

    def ps2ops(t, idx):
        b2 = idx % 2
        ai = idx % 3
        t["ai"] = ai
        si = t["si"]
        mms = list(t["mms"]) + [(negU[:], spb[si][:])]
        rd = [t["hk"], "ident", "negm", "pmask", "negU", "negO", ("spb", si)]
        if t["G"] is not None:
            mms.append((negO[:], t["G"][:]))
            rd.append(t["gkey"])

        def mm2(e, mms=mms, b2=b2):
            rr = None
            for j, (l_, r_) in enumerate(mms):
                rr = e.matmul(ps2[:, b2, :], lhsT=l_, rhs=r_, start=(j == 0), stop=(j == len(mms) - 1))
            return rr
        p.op("pe", mm2, reads=rd, writes=[("ps2", b2)])
        p.op("act", lambda e, b2=b2, ai=ai: e.activation(out=ab[ai][:], in_=ps2[:, b2, :], func=AF.Exp, scale=scale),
             reads=[("ps2", b2)], writes=[("ab", ai)])

    def avops(t, idx):
        h, gq, kb, i, ai = t["h"], t["gq"], t["kb"], t["i"], t["ai"]
        ob = (h * 2 + gq) % 2
        p.op("pe", lambda e: e.matmul(pso[:, ob, :], lhsT=vh[i][:, kb, :], rhs=ab[ai][:], start=t["first"], stop=t["last"]),
             reads=[t["hk"], ("ab", ai)], writes=[("pso", ob)])
        if t["last"]:
            oi = h % 2
            p.op("dve", lambda e: e.tensor_copy(out=ost[oi][:, gq * 512:(gq + 1) * 512], in_=pso[:, ob, :]),
                 reads=[("pso", ob)], writes=[("ost", oi, gq)])
            if gq == 1:
                o = p.op("sp", lambda e: e.dma_start(out=osbT[h * 128:(h + 1) * 128, :], in_=ost[oi][:]),
                         reads=[("ost", oi, 0), ("ost", oi, 1)], dma_slot=("ost", oi))
                c.finals.append(o)

    n = len(tiles)
    head_slot[0] = load_head(0)
    head_slot[1] = load_head(1)
    gens = []
    if side:
        psx = PsRing(c.ps(2, "psx"), 2)
        gens = [mk(c, psx) for mk in side]
    nside = 2 if has_prev else 1

    def side_step():
        while gens:
            try:
                next(gens[0])
                return
            except StopIteration:
                gens.pop(0)
    for step in range(n + 2):
        if step % nside == 0:
            side_step()
        if step < n:
            zops(tiles[step], step)
        if 0 <= step - 1 < n:
            ps2ops(tiles[step - 1], step - 1)
        if 0 <= step - 2 < n:
            t = tiles[step - 2]
            avops(t, step - 2)
            if t["last"] and t["gq"] == 1 and t["h"] + 2 < 8:
                head_slot[t["h"] + 2] = load_head(t["h"] + 2)
    while gens:
        side_step()
    c.finish()


def gen_gmlp(c, p, psx, uT, vn, w_sT, b_s, ogmT, groups=(0, 1)):
    wsb = c.sb([128, 8, 128], BF16, "wsb")
    bbc = c.sb([128, 8, 128], F32, "bbc")
    vsb = c.sb([128, 8, 1024], BF16, "vsb")
    usb = c.sb([128, 8, T], BF16, "usb")
    tmp = [c.sb([128, 512], F32, "tmp%d" % i) for i in range(2)]
    ost = [c.sb([128, T], BF16, "ost%d" % i) for i in range(2)]
    p.op("pool", lambda e: e.dma_start(out=wsb[:], in_=w_sT.rearrange("g s t -> s g t")), writes=["wsb"], dma_slot="wsb")
    p.op("dve", lambda e: e.memset(wsb[64:128, :, 0:64], 0.0), reads=["wsb"], writes=["wsb"])
    p.op("sp", lambda e: e.dma_start(out=bbc[:].rearrange("p g t -> p (g t)"), in_=b_s.rearrange("g t -> (g t)").partition_broadcast(128)),
         writes=["bbc"], dma_slot="bbc")
    p.op("sp", lambda e: e.dma_start(out=vsb[:], in_=vn.rearrange("(c p) f -> p c f", p=128)), writes=["vsb"], dma_slot="vsb")
    p.op("sp", lambda e: e.dma_start(out=usb[:], in_=uT.rearrange("(g p) t -> p g t", p=128)), writes=["usb"], dma_slot="usb")
    yield
    nb = 0
    for g in range(8):
        oi = g % 2
        for half in groups:
            b = psx.next()
            nb += 1

            def mm(e, b=b, g=g, half=half):
                r = None
                for cl in range(4):
                    cblk = half * 4 + cl
                    r = e.matmul(psx.t[:, b, cl * 128:(cl + 1) * 128], lhsT=vsb[:, cblk, g * 128:(g + 1) * 128], rhs=wsb[:, g, :],
                                 start=True, stop=True)
                return r
            p.op("pe", mm, reads=["vsb", "wsb"], writes=[("@psx", b)])
            ti = nb % 2
            p.op("dve", lambda e, b=b, g=g, ti=ti: e.tensor_tensor(
                out=tmp[ti][:].rearrange("p (c t) -> p c t", c=4), in0=psx.t[:, b, :].rearrange("p (c t) -> p c t", c=4),
                in1=bbc[:, g:g + 1, :].to_broadcast([128, 4, 128]), op=ALU.add),
                reads=[("@psx", b), "bbc"], writes=[("tmp", ti)])
            p.op("dve", lambda e, g=g, ti=ti, oi=oi, half=half: e.tensor_tensor(
                out=ost[oi][:, half * 512:(half + 1) * 512], in0=tmp[ti][:], in1=usb[:, g, half * 512:(half + 1) * 512], op=ALU.mult),
                reads=[("tmp", ti), "usb"], writes=[("ost", oi, half)])
            yield
        o = p.op("sp", lambda e, oi=oi, g=g: e.dma_start(out=ogmT[g * 128:(g + 1) * 128, :], in_=ost[oi][:]),
                 reads=[("ost", oi, hf) for hf in groups], dma_slot=("ost", oi))
        c.finals.append(o)


class PsRing:
    def __init__(self, t, n):
        self.t = t
        self.n = n
        self.i = 0

    def next(self):
        b = self.i % self.n
        self.i += 1
        return b


def phase_gmlp(nc, uT, vn, w_sT, b_s, ogmT):
    c = Ctx(nc)
    psx = PsRing(c.ps(4, "psx"), 4)
    for _ in gen_gmlp(c, TP(c.p, "gm"), psx, uT, vn, w_sT, b_s, ogmT):
        pass
    c.finish()


def gen_xattn(c, p, psx, memT, g_mem, w_kv, qxT, oxaT, groups=(0, 1)):
    ones, epsb = consts(c, p)
    M = 256
    mt = c.sb([128, KC, M], F32, "mt")
    mn = c.sb([128, KC, M], BF16, "mn")
    gsb = c.sb([128, KC], F32, "gsb")
    scr = c.sb([128, KC, 256], BF16, "scr")
    rstd = c.sb([128, M], F32, "rstd")
    kmT = c.sb([128, 8, M], BF16, "kmT")
    vm = c.sb([128, 2, 1024], BF16, "vm")
    qx = c.sb([128, 8, T], BF16, "qx")
    eT = [c.sb([128, 2, 512], BF16, "eT%d" % i) for i in range(2)]
    rden = [c.sb([128, 512], F32, "rden%d" % i) for i in range(2)]
    ost = [c.sb([128, T], BF16, "ost%d" % i) for i in range(2)]
    ps = psx.t
    ws = WStream(c, "w", KC, 512, 2, p=p)
    p.op("sp", lambda e: e.dma_start(out=mt[:], in_=memT.rearrange("(kc p) m -> p kc m", p=128)), writes=["mt"], dma_slot="mt")
    p.op("sp", lambda e: e.dma_start(out=gsb[:], in_=g_mem), writes=["gsb"], dma_slot="gsb")
    p.op("sp", lambda e: e.dma_start(out=qx[:], in_=qxT.rearrange("(g p) t -> p g t", p=128)), writes=["qx"], dma_slot="qx")
    b0 = psx.next()
    rms_stats(c, mt[:], M, ps[:, b0, :], ones, epsb, scr, rstd[:], ["mt", "ones", "epsb"], "rstd", p=p, pskey=("@psx", b0))
    yield
    for k in range(KC):
        p.op("dve", lambda e, k=k: e.scalar_tensor_tensor(out=mn[:, k, :], in0=mt[:, k, :], scalar=gsb[:, k:k + 1], in1=rstd[:],
                                                          op0=ALU.mult, op1=ALU.mult),
             reads=["mt", "gsb", "rstd"], writes=[("mn", k)])
    yield
    mnk = [("mn", k) for k in range(KC)]
    nb = 0
    for (s, (buf, wkey)) in stream(range(4), lambda s: ws.load(w_kv[:, s * 512:(s + 1) * 512]), 2):
        if s < 2:
            for cc in range(4):
                b = psx.next()

                def mm(e, b=b, cc=cc, buf=buf):
                    r = None
                    for k in range(KC):
                        r = e.matmul(ps[:, b, 0:M], lhsT=buf[:, k, cc * 128:(cc + 1) * 128], rhs=mn[:, k, :], start=(k == 0), stop=(k == KC - 1))
                    return r
                p.op("pe", mm, reads=[wkey[cc]] + mnk, writes=[("@psx", b)])
                p.op("act", lambda e, b=b, cc=cc, s=s: e.activation(out=kmT[:, s * 4 + cc, :], in_=ps[:, b, 0:M], func=AF.Copy),
                     reads=[("@psx", b)], writes=[("kmT", s * 4 + cc)])
                yield
        else:
            for mc in range(2):
                b = psx.next()

                def mm(e, b=b, mc=mc, buf=buf):
                    r = None
                    for k in range(KC):
                        r = e.matmul(ps[:, b, :], lhsT=mn[:, k, mc * 128:(mc + 1) * 128], rhs=buf[:, k, :], start=(k == 0), stop=(k == KC - 1))
                    return r
                p.op("pe", mm, reads=list(wkey) + mnk, writes=[("@psx", b)])
                p.op("act", lambda e, b=b, mc=mc, s=s: e.activation(out=vm[:, mc, (s - 2) * 512:(s - 1) * 512], in_=ps[:, b, :], func=AF.Copy),
                     reads=[("@psx", b)], writes=[("vm", mc, s - 2)])
                yield
    vmk = [("vm", mc, s) for mc in range(2) for s in range(2)]
    it = 0
    for hh in range(4):
        for g in groups:
            ei = it % 2
            it += 1
            for mc in range(2):
                b = psx.next()

                def mm(e, b=b, mc=mc, hh=hh, g=g):
                    r = None
                    for dc in range(2):
                        r = e.matmul(ps[:, b, :], lhsT=kmT[:, 2 * hh + dc, mc * 128:(mc + 1) * 128], rhs=qx[:, 2 * hh + dc, g * 512:(g + 1) * 512],
                                     start=(dc == 0), stop=(dc == 1))
                    return r
                p.op("pe", mm, reads=[("kmT", 2 * hh), ("kmT", 2 * hh + 1), "qx"], writes=[("@psx", b)])
                p.op("act", lambda e, b=b, mc=mc, ei=ei: e.activation(out=eT[ei][:, mc, :], in_=ps[:, b, :], func=AF.Exp, scale=1.0 / 16),
                     reads=[("@psx", b)], writes=[("eT", ei, mc)])
                yield
            di = it % 2
            bd = psx.next()

            def mmd(e, ei=ei, bd=bd):
                r = None
                for mc in range(2):
                    r = e.matmul(ps[:, bd, :], lhsT=ones[:], rhs=eT[ei][:, mc, :], start=(mc == 0), stop=(mc == 1))
                return r
            p.op("pe", mmd, reads=[("eT", ei, 0), ("eT", ei, 1), "ones"], writes=[("@psx", bd)])
            p.op("dve", lambda e, di=di, bd=bd: e.reciprocal(out=rden[di][:], in_=ps[:, bd, :]), reads=[("@psx", bd)], writes=[("rden", di)])
            yield
            for dc in range(2):
                b = psx.next()
                ch = 2 * hh + dc
                oi = ch % 2

                def mmo(e, b=b, ei=ei, ch=ch):
                    r = None
                    for mc in range(2):
                        r = e.matmul(ps[:, b, :], lhsT=vm[:, mc, ch * 128:(ch + 1) * 128], rhs=eT[ei][:, mc, :], start=(mc == 0), stop=(mc == 1))
                    return r
                p.op("pe", mmo, reads=vmk + [("eT", ei, 0), ("eT", ei, 1)], writes=[("@psx", b)])
                p.op("dve", lambda e, b=b, oi=oi, g=g, di=di: e.tensor_tensor(out=ost[oi][:, g * 512:(g + 1) * 512], in0=ps[:, b, :], in1=rden[di][:], op=ALU.mult),
                     reads=[("@psx", b), ("rden", di)], writes=[("ost", oi, g)])
                yield
        for dc in range(2):
            ch = 2 * hh + dc
            oi = ch % 2
            o = p.op("sp", lambda e, oi=oi, ch=ch: e.dma_start(out=oxaT[ch * 128:(ch + 1) * 128, :], in_=ost[oi][:]),
                     reads=[("ost", oi, g) for g in groups], dma_slot=("ost", oi))
            c.finals.append(o)


def phase_xattn(nc, memT, g_mem, w_kv, qxT, oxaT):
    c = Ctx(nc)
    psx = PsRing(c.ps(4, "psx"), 4)
    for _ in gen_xattn(c, TP(c.p, "xa"), psx, memT, g_mem, w_kv, qxT, oxaT):
        pass
    c.finish()


def phase_merge(nc, hT, obrT, w_gate, b_gate, w_br, mT, groups=(0, 1)):
    c = Ctx(nc)
    p = c.p
    ht = c.sb([128, KC, T], BF16, "ht")
    ob = [c.sb([128, 8, T], BF16, "ob%d" % i) for i in range(3)]
    bg = c.sb([128, 48], F32, "bg")
    acc = c.sb([128, 8, 512], F32, "acc")
    sig = [c.sb([128, 512], F32, "sig%d" % i) for i in range(2)]
    tmp = [c.sb([128, 512], F32, "tmp%d" % i) for i in range(2)]
    ost = [c.sb([128, T], BF16, "ost%d" % i) for i in range(2)]
    psg = c.ps(3, "psg")
    psb = c.ps(3, "psb")
    wg = WStream(c, "wg", KC, 512, 2)
    wb = WStream(c, "wb", 8, 512, 2)
    p.op("sp", lambda e: e.dma_start(out=ht[:], in_=hT.rearrange("(kc p) t -> p kc t", p=128)), writes=["ht"], dma_slot="ht")
    for i in range(3):
        p.op("act" if i % 2 == 0 else "sp", lambda e, i=i: e.dma_start(out=ob[i][:], in_=obrT[i].rearrange("(kc p) t -> p kc t", p=128)), writes=[("ob", i)], dma_slot=("ob", i))
    p.op("sp", lambda e: e.dma_start(out=bg[:], in_=b_gate), writes=["bg"], dma_slot="bg")
    items = [(s, br) for s in range(4) for br in range(3)]

    def loader(it):
        s, br = it
        return (wg.load(w_gate[:, br * 2048 + s * 512: br * 2048 + (s + 1) * 512]), wb.load(w_br[br][:, s * 512:(s + 1) * 512]))
    n = 0
    for ((s, br), ((gbuf, gkey), (bbuf, bkey))) in stream(items, loader, 2):
        for cc in range(4):
            fch = s * 4 + cc
            oi = fch % 2
            for g in groups:
                b = n % 3
                n += 1

                def mmg(e, b=b, cc=cc, g=g, gbuf=gbuf):
                    r = None
                    for k in range(KC):
                        r = e.matmul(psg[:, b, :], lhsT=gbuf[:, k, cc * 128:(cc + 1) * 128], rhs=ht[:, k, g * 512:(g + 1) * 512], start=(k == 0), stop=(k == KC - 1))
                    return r
                p.op("pe", mmg, reads=[gkey[cc], "ht"], writes=[("psg", b)])

                def mmb(e, b=b, cc=cc, g=g, bbuf=bbuf, br=br):
                    r = None
                    for k in range(8):
                        r = e.matmul(psb[:, b, :], lhsT=bbuf[:, k, cc * 128:(cc + 1) * 128], rhs=ob[br][:, k, g * 512:(g + 1) * 512], start=(k == 0), stop=(k == 7))
                    return r
                p.op("pe", mmb, reads=[bkey[cc], ("ob", br)], writes=[("psb", b)])
                si = n % 2
                col = br * 16 + fch
                p.op("act", lambda e, b=b, si=si, col=col: e.activation(out=sig[si][:], in_=psg[:, b, :], func=AF.Sigmoid, bias=bg[:, col:col + 1], scale=1.0),
                     reads=[("psg", b), "bg"], writes=[("sig", si)])
                ak = ("acc", cc, g)
                asl = acc[:, cc * 2 + g, :]
                if br == 0:
                    p.op("dve", lambda e, b=b, si=si, asl=asl: e.tensor_tensor(out=asl, in0=psb[:, b, :], in1=sig[si][:], op=ALU.mult),
                         reads=[("psb", b), ("sig", si)], writes=[ak])
                else:
                    p.op("dve", lambda e, b=b, si=si: e.tensor_tensor(out=tmp[si][:], in0=psb[:, b, :], in1=sig[si][:], op=ALU.mult),
                         reads=[("psb", b), ("sig", si)], writes=[("tmp", si)])
                    if br == 1:
                        p.op("pool", lambda e, si=si, asl=asl: e.tensor_tensor(out=asl, in0=asl, in1=tmp[si][:], op=ALU.add),
                             reads=[("tmp", si), ak], writes=[ak])
                    else:
                        p.op("pool", lambda e, si=si, asl=asl, oi=oi, g=g: e.tensor_tensor(out=ost[oi][:, g * 512:(g + 1) * 512], in0=asl, in1=tmp[si][:], op=ALU.add),
                             reads=[("tmp", si), ak], writes=[("ost", oi, g)])
            if br == 2:
                o = p.op("sp", lambda e, oi=oi, fch=fch: e.dma_start(out=mT[fch * 128:(fch + 1) * 128, :], in_=ost[oi][:]),
                         reads=[("ost", oi, g) for g in groups], dma_slot=("ost", oi))
                c.finals.append(o)
    c.finish()


def phase_proj_norm_res(nc, inT, kc_in, w, g_post, resT, outT, cw, res_cols0=0, groups=(0, 1)):
    c = Ctx(nc)
    p = c.p
    ones, epsb = consts(c)
    it = c.sb([128, kc_in, T], BF16, "it")
    y = c.sb([128, KC, T], F32, "y")
    sq = [c.sb([128, 512], BF16, "sq%d" % i) for i in range(3)]
    gsb = c.sb([128, KC], F32, "gsb")
    rstd = c.sb([128, 2, 512], F32, "rstd")
    nwb = 4 if kc_in <= 16 else 3
    nxr = 4 if kc_in <= 16 else 3
    xr = [c.sb([128, T], F32, "xr%d" % i) for i in range(nxr)]
    ps = c.ps(4, "ps")
    pss = c.ps(2, "pss")
    ws = WStream(c, "w", kc_in, cw, nwb)
    NPC = 4
    bnd = [kc_in * i // NPC for i in range(NPC + 1)]
    for i in range(NPC):
        p.op("sp" if i % 2 == 0 else "act", lambda e, i=i: e.dma_start(out=it[:, bnd[i]:bnd[i + 1], :], in_=inT[bnd[i] * 128:bnd[i + 1] * 128, :].rearrange("(kc p) t -> p kc t", p=128)),
             writes=[("it", i)], dma_slot=("it", i))
    p.op("sp", lambda e: e.dma_start(out=gsb[:], in_=g_post), writes=["gsb"], dma_slot="gsb")
    nslab = 2048 // cw
    cpers = cw // 128
    n = 0
    pend = []
    for (s, (buf, wkey)) in stream(range(nslab), lambda s: ws.load(w[:, s * cw:(s + 1) * cw]), nwb):
        for cc in range(cpers):
            fch = s * cpers + cc
            for g in groups:
                b = n % 4
                n += 1

                for i in range(NPC):
                    def mm(e, b=b, cc=cc, g=g, buf=buf, i=i):
                        r = None
                        for k in range(bnd[i], bnd[i + 1]):
                            r = e.matmul(ps[:, b, :], lhsT=buf[:, k, cc * 128:(cc + 1) * 128], rhs=it[:, k, g * 512:(g + 1) * 512], start=(k == 0), stop=(k == kc_in - 1))
                        return r
                    p.op("pe", mm, reads=[wkey[cc], ("it", i)], writes=[("ps", b)])
                p.op("act", lambda e, b=b, fch=fch, g=g: e.activation(out=y[:, fch, g * 512:(g + 1) * 512], in_=ps[:, b, :], func=AF.Copy),
                     reads=[("ps", b)], writes=[("y", fch, g)])
                qi = n % 3
                p.op("act", lambda e, b=b, qi=qi: e.activation(out=sq[qi][:], in_=ps[:, b, :], func=AF.Square),
                     reads=[("ps", b)], writes=[("sq", qi)])
                if pend:
                    pend.pop()()
                pend.append(lambda qi=qi, g=g, fch=fch: p.op(
                    "pe", lambda e: e.matmul(pss[:, g, :], lhsT=ones[:], rhs=sq[qi][:], start=(fch == 0), stop=(fch == KC - 1)),
                    reads=[("sq", qi), "ones"], writes=[("pss", g)]))
    while pend:
        pend.pop()()
    for g in groups:
        p.op("act", lambda e, g=g: e.activation(out=rstd[:, g, :], in_=pss[:, g, :], func=AF.Ln, bias=epsb[:], scale=1.0 / D),
             reads=[("pss", g), "epsb"], writes=[("rstd", g)])
        p.op("act", lambda e, g=g: e.activation(out=rstd[:, g, :], in_=rstd[:, g, :], func=AF.Exp, scale=-0.5), reads=[("rstd", g)], writes=[("rstd", g)])
    for fch in range(KC):
        xi = fch % nxr
        p.op("sp", lambda e, xi=xi, fch=fch: e.dma_start(out=xr[xi][:], in_=resT[fch * 128:(fch + 1) * 128, res_cols0:res_cols0 + T]),
             writes=[("xr", xi)], dma_slot=("xr", xi))
        for g in groups:
            p.op("dve", lambda e, fch=fch, g=g: e.scalar_tensor_tensor(out=y[:, fch, g * 512:(g + 1) * 512], in0=y[:, fch, g * 512:(g + 1) * 512],
                                                                      scalar=gsb[:, fch:fch + 1], in1=rstd[:, g, :], op0=ALU.mult, op1=ALU.mult),
                 reads=[("y", fch, g), "gsb", ("rstd", g)], writes=[("y", fch, g)])
            p.op("dve" if g == 0 else "pool", lambda e, fch=fch, g=g, xi=xi: e.tensor_tensor(
                out=y[:, fch, g * 512:(g + 1) * 512], in0=y[:, fch, g * 512:(g + 1) * 512], in1=xr[xi][:, g * 512:(g + 1) * 512], op=ALU.add),
                reads=[("y", fch, g), ("xr", xi)], writes=[("y", fch, g)])
        o = p.op("act", lambda e, fch=fch: e.dma_start(out=outT[fch * 128:(fch + 1) * 128, :], in_=y[:, fch, :]),
                 reads=[("y", fch, g) for g in groups], dma_slot=("yo", fch % 4))
        c.finals.append(o)
    c.finish()


def phase_ffn_up(nc, xmT_h, g_pre, w_up, conv_w, conv_b, actT, xmT=None, haloT=None, hv=None):
    c = Ctx(nc)
    p = c.p
    ones, epsb = consts(c)
    TH = T + 2
    xh = c.sb([128, KC, TH], F32, "xh")
    h2 = c.sb([128, KC, TH], BF16, "h2")
    gsb = c.sb([128, KC], F32, "gsb")
    cwsb = c.sb([128, 44, 3], F32, "cwsb")
    cbsb = c.sb([128, 44], F32, "cbsb")
    scr = c.sb([128, KC, 512], BF16, "scr")
    rstd = c.sb([128, TH], F32, "rstd")
    gs = [c.sb([128, TH], F32, "gs%d" % i) for i in range(2)]
    cv = [c.sb([128, T], F32, "cv%d" % i) for i in range(2)]
    gl = [c.sb([128, T], F32, "gl%d" % i) for i in range(2)]
    ost = [c.sb([128, T], BF16, "ost%d" % i) for i in range(2)]
    psg = c.ps(3, "psg")
    psv = c.ps(3, "psv")
    pss = c.ps(1, "pss")
    CW = 256
    wg = WStream(c, "wg", KC, CW, 2)
    wv = WStream(c, "wv", KC, CW, 2)
    if xmT_h is not None:
        p.op("sp", lambda e: e.dma_start(out=xh[:], in_=xmT_h.rearrange("(kc p) t -> p kc t", p=128)), writes=["xh"], dma_slot="xh")
    else:
        p.op("sp", lambda e: e.dma_start(out=xh[:, :, 2:TH], in_=xmT.rearrange("(kc p) t -> p kc t", p=128)), writes=["xh"], dma_slot="xh")
        if haloT is not None:
            p.op("sp", lambda e: e.dma_start(out=xh[:, :, 0:2], in_=haloT.rearrange("(kc p) t -> p kc t", p=128)), writes=["xhh"], dma_slot="xhh")
            if hv is not None:
                hvsb = c.sb([128, 1], F32, "hvsb")
                p.op("sp", lambda e: e.dma_start(out=hvsb[:], in_=hv), writes=["hvsb"], dma_slot="hvsb")
                p.op("dve", lambda e: e.tensor_scalar(out=xh[:, :, 0:2], in0=xh[:, :, 0:2], scalar1=hvsb[:, 0:1], scalar2=None, op0=ALU.mult),
                     reads=["xhh", "hvsb"], writes=["xhh"])
        else:
            p.op("dve", lambda e: e.memset(xh[:, :, 0:2], 0.0), writes=["xhh"])
    p.op("sp", lambda e: e.dma_start(out=gsb[:], in_=g_pre), writes=["gsb"], dma_slot="gsb")
    p.op("act", lambda e: e.dma_start(out=cwsb[:], in_=conv_w), writes=["cwsb"], dma_slot="cwsb")
    p.op("act", lambda e: e.dma_start(out=cbsb[:], in_=conv_b), writes=["cbsb"], dma_slot="cbsb")
    segs = [(0, 2), (2, 512), (514, 512)]
    zero_halo = (xmT_h is None and haloT is None)
    for si, (c0, n) in enumerate(segs):
        if si == 0 and zero_halo:
            continue
        rms_stats(c, xh[:, :, c0:c0 + n], n, pss[:, 0, :], ones, epsb, scr, rstd[:, c0:c0 + n], ["xh", "xhh", "ones", "epsb"], ("rstd", si))
        for k in range(KC):
            p.op("dve", lambda e, k=k, c0=c0, n=n: e.scalar_tensor_tensor(
                out=h2[:, k, c0:c0 + n], in0=xh[:, k, c0:c0 + n], scalar=gsb[:, k:k + 1], in1=rstd[:, c0:c0 + n], op0=ALU.mult, op1=ALU.mult),
                reads=["xh", "xhh", "gsb", ("rstd", si)], writes=[("h2", si, k)])
    h2k = [("h2", si, k) for si in range(1 if zero_halo else 0, 3) for k in range(KC)]
    nslab = 5632 // CW
    cpers = CW // 128

    def loader(s):
        return (wg.load(w_up[:, s * CW:(s + 1) * CW]), wv.load(w_up[:, 5632 + s * CW: 5632 + (s + 1) * CW]))
    n = 0
    for (s, ((gbuf, gkey), (vbuf, vkey))) in stream(range(nslab), loader, 2):
        for cc in range(cpers):
            j = s * cpers + cc
            ji = j % 2
            gkeys = []
            for si, (c0, nn) in enumerate(segs):
                if si == 0 and zero_halo:
                    p.op("pool", lambda e, ji=ji: e.memset(gs[ji][:, 0:2], 0.0), writes=[("gs", ji, 0)])
                    gkeys.append(("gs", ji, 0))
                    continue
                b = n % 3
                n += 1

                def mmg(e, b=b, cc=cc, c0=c0, nn=nn, gbuf=gbuf):
                    r = None
                    for k in range(KC):
                        r = e.matmul(psg[:, b, 0:nn], lhsT=gbuf[:, k, cc * 128:(cc + 1) * 128], rhs=h2[:, k, c0:c0 + nn], start=(k == 0), stop=(k == KC - 1))
                    return r
                p.op("pe", mmg, reads=[gkey[cc]] + [("h2", si, k) for k in range(KC)], writes=[("psg", b)])
                p.op("act", lambda e, b=b, ji=ji, c0=c0, nn=nn: e.activation(out=gs[ji][:, c0:c0 + nn], in_=psg[:, b, 0:nn], func=AF.Copy),
                     reads=[("psg", b)], writes=[("gs", ji, si)])
                gkeys.append(("gs", ji, si))
            p.op("dve", lambda e, ji=ji, j=j: e.tensor_scalar(out=cv[ji][:], in0=gs[ji][:, 2:2 + T], scalar1=cwsb[:, j, 2:3], scalar2=cbsb[:, j:j + 1],
                                                              op0=ALU.mult, op1=ALU.add),
                 reads=gkeys + ["cwsb", "cbsb"], writes=[("cv", ji)])
            p.op("dve", lambda e, ji=ji, j=j: e.scalar_tensor_tensor(out=cv[ji][:], in0=gs[ji][:, 1:1 + T], scalar=cwsb[:, j, 1:2], in1=cv[ji][:],
                                                                     op0=ALU.mult, op1=ALU.add),
                 reads=gkeys + ["cwsb", ("cv", ji)], writes=[("cv", ji)])
            p.op("dve", lambda e, ji=ji, j=j: e.scalar_tensor_tensor(out=cv[ji][:], in0=gs[ji][:, 0:T], scalar=cwsb[:, j, 0:1], in1=cv[ji][:],
                                                                     op0=ALU.mult, op1=ALU.add),
                 reads=gkeys + ["cwsb", ("cv", ji)], writes=[("cv", ji)])
            p.op("act", lambda e, ji=ji: e.activation(out=gl[ji][:], in_=cv[ji][:], func=AF.Gelu_apprx_tanh), reads=[("cv", ji)], writes=[("gl", ji)])
            for g in range(2):
                b = n % 3
                n += 1

                def mmv(e, b=b, cc=cc, g=g, vbuf=vbuf):
                    r = None
                    for k in range(KC):
                        r = e.matmul(psv[:, b, :], lhsT=vbuf[:, k, cc * 128:(cc + 1) * 128], rhs=h2[:, k, 2 + g * 512: 2 + (g + 1) * 512], start=(k == 0), stop=(k == KC - 1))
                    return r
                p.op("pe", mmv, reads=[vkey[cc]] + [("h2", g + 1, k) for k in range(KC)], writes=[("psv", b)])
                p.op("dve", lambda e, b=b, ji=ji, g=g: e.tensor_tensor(out=ost[ji][:, g * 512:(g + 1) * 512], in0=psv[:, b, :], in1=gl[ji][:, g * 512:(g + 1) * 512], op=ALU.mult),
                     reads=[("psv", b), ("gl", ji)], writes=[("ost", ji, g)])
            o = p.op("sp", lambda e, ji=ji, j=j: e.dma_start(out=actT[j * 128:(j + 1) * 128, :], in_=ost[ji][:]),
                     reads=[("ost", ji, 0), ("ost", ji, 1)], dma_slot=("ost", ji))
            c.finals.append(o)
    c.finish()


import ml_dtypes
from concourse.bass_utils import run_bass_kernel_spmd

_BF = ml_dtypes.bfloat16
_NC = [None]
S = 2048


def _build(L=2, dbg=False):
    nc = bass.Bass("TRN2", target_bir_lowering=False)
    I = lambda n, s, dt=F32: dram(nc, n, s, dt, "ExternalInput")
    N = lambda n, s, dt=BF16: dram(nc, n, s, dt, "ExternalOutput" if dbg else "Internal")
    xT = I("xT", [2048, S])
    memT = I("memT", [2048, 256])
    w_in = I("w_in", [L, 2048, 6144])
    w_kv = I("w_kv", [L, 2048, 2048])
    w_gate = I("w_gate", [L, 2048, 6144])
    w_br = [I("w_br%d" % i, [L, 1024, 2048]) for i in range(3)]
    w_out = I("w_out", [L, 2048, 2048])
    w_up = I("w_up", [L, 2048, 11264])
    w_down = I("w_down", [L, 5632, 2048])
    g_mix_pre = I("g_mix_pre", [L, 128, 16])
    gvn = I("gvn", [L, 1024])
    w_sT = I("w_sT", [L, 8, 128, 128])
    b_s = I("b_s", [L, 8, 128])
    g_mem = I("g_mem", [L, 128, 16])
    b_gate = I("b_gate", [L, 128, 48])
    g_mix_post = I("g_mix_post", [L, 128, 16])
    g_ffn_pre = I("g_ffn_pre", [L, 128, 16])
    conv_w = I("conv_w", [L, 128, 44, 3])
    conv_b = I("conv_b", [L, 128, 44])
    g_ffn_post = I("g_ffn_post", [L, 128, 16])
    outT = dram(nc, "outT", [2048, T], F32, "ExternalOutput")

    x1T = N("x1T", [2048, S], F32)
    xmT = [N("xmT%d" % l, [2048, S], F32) for l in range(2)]
    hT = N("hT", [2048, T])
    qT = N("qT", [1024, T])
    uT = N("uT", [1024, T])
    vn = N("vn", [T, 1024])
    qxT = N("qxT", [1024, T])
    osbT = N("osbT", [1024, T])
    ogmT = N("ogmT", [1024, T])
    oxaT = N("oxaT", [1024, T])
    mT = N("mT", [2048, T])
    actT = N("actT", [5632, T])
    kT = [[N("kT%d%d" % (l, h), [1024, T]) for h in range(2)] for l in range(2)]
    vv = [[N("v%d%d" % (l, h), [T, 1024]) for h in range(2)] for l in range(2)]

    sel = I("sel", [128, 2])
    pm = I("pm", [128, 512], BF16)
    hv = I("hv", [128, 1])
    xownT = N("xownT", [2048, T], F32)
    xmownT = N("xmownT", [2048, T], F32)
    kTo = N("kTo", [1024, T])
    vo = N("vo", [T, 1024])

    def mixer(l, xin_h, kT_own, v_own, kT_prev, v_prev, has_prev, prevmask, res, xm_out, blend=None, groups=(0, 1)):
        if blend is None:
            phase_proj(nc, xin_h, g_mix_pre[l], w_in[l], gvn[l], hT, qT, kT_own, v_own, uT, vn, qxT, groups=groups)
        else:
            phase_proj(nc, xin_h, g_mix_pre[l], w_in[l], gvn[l], hT, qT, kT_own, v_own, uT, vn, qxT, xT_alt=blend, sel=sel, xownT=xownT)
        side = [lambda c, psx: gen_xattn(c, TP(c.p, "xa"), psx, memT, g_mem[l], w_kv[l], qxT, oxaT, groups=groups),
                lambda c, psx: gen_gmlp(c, TP(c.p, "gm"), psx, uT, vn, w_sT[l], b_s[l], ogmT, groups=groups)]
        phase_attn2(nc, qT, kT_own, kT_prev, v_own, v_prev, prevmask, osbT, has_prev=has_prev, side=side, groups=groups)
        phase_merge(nc, hT, [osbT, ogmT, oxaT], w_gate[l], b_gate[l], [w_br[i][l] for i in range(3)], mT, groups=groups)
        phase_proj_norm_res(nc, mT, 16, w_out[l], g_mix_post[l], res, xm_out, 512, groups=groups)

    def ffn(l, xm_h, halo, hv_, out_h):
        phase_ffn_up(nc, None, g_ffn_pre[l], w_up[l], conv_w[l], conv_b[l], actT, xmT=xm_h, haloT=halo, hv=hv_)
        phase_proj_norm_res(nc, actT, 44, w_down[l], g_ffn_post[l], xm_h, out_h, 128)

    h0 = slice(0, T)
    h1 = slice(T, 2 * T)
    l = 0
    for h, tsl in enumerate((h0, h1)):
        mixer(l, xT[:, tsl], kT[l][h], vv[l][h], kT[l][0], vv[l][0], h == 1, None, xT[:, tsl], xmT[l][:, tsl])
        ffn(l, xmT[l][:, tsl], (xmT[l][:, T - 2:T] if h == 1 else None), None, x1T[:, tsl])
    l = 1
    mixer(l, x1T[:, h0], kT[l][0], vv[l][0], kT[l][0], vv[l][0], False, None, x1T[:, h0], xmT[l][:, h0], groups=(1,))
    mixer(l, x1T[:, h0], kTo, vo, kT[l][0], vv[l][0], True, pm, xownT, xmownT, blend=x1T[:, h1])
    ffn(l, xmownT, xmT[l][:, T - 2:T], hv, outT)
    return nc


def _pl(v, n):
    v = np.asarray(v, dtype=np.float32)
    return np.ascontiguousarray(v.reshape(v.shape[0], n, 128).transpose(0, 2, 1))


def kernel(**inp):
    inp = {k: np.asarray(v) for k, v in inp.items()}
    x = inp["x"]
    mem = inp["mem"]
    if _NC[0] is None:
        _NC[0] = _build()
    nc = _NC[0]
    f32 = lambda a: np.ascontiguousarray(np.asarray(a, dtype=np.float32))
    shared = {
        "w_in": f32(inp["w_in"]), "w_kv": f32(inp["w_mem_kv"]), "w_gate": f32(inp["w_gate"]),
        "w_br0": f32(inp["w_br_sb"]), "w_br1": f32(inp["w_br_gm"]), "w_br2": f32(inp["w_br_xa"]),
        "w_out": f32(inp["w_out"]), "w_up": f32(inp["w_up"]), "w_down": f32(inp["w_down"]),
        "g_mix_pre": _pl(inp["g_mix_pre"], 16), "gvn": f32(inp["g_vnorm"]),
        "w_sT": np.ascontiguousarray(inp["w_s"].transpose(0, 1, 3, 2)), "b_s": f32(inp["b_s"]),
        "g_mem": _pl(inp["g_mem"], 16), "b_gate": _pl(inp["b_gate"], 48), "g_mix_post": _pl(inp["g_mix_post"], 16),
        "g_ffn_pre": _pl(inp["g_ffn_pre"], 16),
        "conv_w": np.ascontiguousarray(inp["conv_w"].reshape(2, 3, 44, 128).transpose(0, 3, 2, 1)),
        "conv_b": _pl(inp["conv_b"], 44), "g_ffn_post": _pl(inp["g_ffn_post"], 16),
    }
    in_maps = []
    for c in range(8):
        b, par = divmod(c, 2)
        m = dict(shared)
        m["xT"] = np.ascontiguousarray(x[b].T)
        m["memT"] = np.ascontiguousarray(mem[b].T)
        m["sel"] = np.tile(np.array([[1.0 - par, float(par)]], np.float32), (128, 1))
        m["pm"] = np.full((128, 512), (0.0 if par == 1 else NEG), dtype=np.float32).astype(_BF)
        m["hv"] = np.full((128, 1), float(par), np.float32)
        in_maps.append(m)
    res = run_bass_kernel_spmd(nc, in_maps, core_ids=list(range(8)))
    out = np.empty_like(x)
    for c in range(8):
        b, par = divmod(c, 2)
        out[b, par * T:(par + 1) * T] = res.results[c]["outT"].T
    return out
```

```python
import numpy as np
import concourse.bass as bass
import concourse.mybir as mybir

F32 = mybir.dt.float32
BF16 = mybir.dt.bfloat16
AF = mybir.ActivationFunctionType
ALU = mybir.AluOpType
AX = mybir.AxisListType

ENGS = ("pe", "act", "dve", "pool", "sp")


class Op:
    __slots__ = ("id", "eng", "fn", "deps", "dma_sem", "dma_n", "needs_inc", "ev")

    def __init__(self, id, eng, fn, deps, dma_sem=None, dma_n=0):
        self.id = id
        self.eng = eng
        self.fn = fn
        self.deps = deps
        self.dma_sem = dma_sem
        self.dma_n = dma_n
        self.needs_inc = False
        self.ev = None


class Prog:
    UID = 0

    def __init__(self, nc):
        self.nc = nc
        self.ops = []
        self.last_w = {}
        self.readers = {}
        self.dma_slots = {}
        self.n_dma_sems = 0

    def op(self, eng, fn, reads=(), writes=(), dma_slot=None, dma_n=1):
        strong = set()
        weak = set()
        for k in reads:
            w = self.last_w.get(k)
            if w is not None:
                strong.add(w)
        for k in writes:
            w = self.last_w.get(k)
            if w is not None:
                strong.add(w)
            for r in self.readers.get(k, ()):
                weak.add(r)
        oid = len(self.ops)
        o = Op(oid, eng, fn, None)
        dl = []
        for d in strong | weak:
            do = self.ops[d]
            if do.eng == eng and do.dma_sem is None:
                if eng == "pe" or eng == "sp":
                    continue
                if d not in strong:
                    continue
            dl.append(d)
        o.deps = dl
        if dma_slot is not None:
            if dma_slot not in self.dma_slots:
                self.dma_slots[dma_slot] = [self.n_dma_sems, 0]
                self.n_dma_sems += 1
            s = self.dma_slots[dma_slot]
            s[1] += dma_n
            o.dma_sem = s[0]
            o.dma_n = dma_n
            o.ev = ("d", s[0], 16 * s[1])
        self.ops.append(o)
        for k in reads:
            self.readers.setdefault(k, []).append(oid)
        for k in writes:
            self.last_w[k] = oid
            self.readers[k] = []
        return oid

    def emit(self, final_wait_ops=()):
        nc = self.nc
        ops = self.ops
        for o in ops:
            for d in o.deps:
                ops[d].needs_inc = True
        for d in final_wait_ops:
            ops[d].needs_inc = True
        cnt = {e: 0 for e in ENGS}
        for o in ops:
            if o.dma_sem is None and o.needs_inc:
                cnt[o.eng] += 1
                o.ev = ("e", o.eng, cnt[o.eng])
        per_eng = {e: [o for o in ops if o.eng == e] for e in ENGS}
        Prog.UID += 1
        uid = Prog.UID
        esem = {e: nc.alloc_semaphore(name="s%d_%s" % (uid, e)) for e in ENGS}
        dsem = [nc.alloc_semaphore(name="d%d_%d" % (uid, i)) for i in range(self.n_dma_sems)]
        allsems = list(esem.values()) + dsem
        with nc.Block() as block:

            def semof(ev):
                return esem[ev[1]] if ev[0] == "e" else dsem[ev[1]]

            def run(engname, eng):
                seen = {}
                for o in per_eng[engname]:
                    need = {}
                    for d in o.deps:
                        ev = ops[d].ev
                        key = (ev[0], ev[1])
                        if seen.get(key, 0) >= ev[2]:
                            continue
                        if need.get(key, 0) < ev[2]:
                            need[key] = ev[2]
                    for key, v in need.items():
                        eng.wait_ge(semof((key[0], key[1], v)), v)
                        seen[key] = v
                    r = o.fn(eng)
                    if o.dma_sem is not None:
                        rs = r if isinstance(r, (list, tuple)) else [r]
                        assert len(rs) == o.dma_n, (len(rs), o.dma_n)
                        for ri in rs:
                            ri.then_inc(dsem[o.dma_sem], 16)
                    elif o.needs_inc:
                        r.then_inc(esem[engname], 1)
                if engname == "sp":
                    for d in final_wait_ops:
                        ev = ops[d].ev
                        eng.wait_ge(semof(ev), ev[2])

            @block.tensor
            def _(e):
                run("pe", e)

            @block.scalar
            def _(e):
                run("act", e)

            @block.vector
            def _(e):
                run("dve", e)

            @block.gpsimd
            def _(e):
                run("pool", e)

            @block.sync
            def _(e):
                run("sp", e)
        nc.clear_and_free_semaphores(allsems)
        nc.all_engine_barrier()
import contextlib
import numpy as np
import concourse.bass as bass
import concourse.mybir as mybir

D = 2048
T = 1024
KC = 16
EPS = 1e-6
NEG = -30000.0
SQ128 = 11.3125


def dram(nc, name, shape, dt, kind):
    return nc.dram_tensor(name, list(shape), dt, kind=kind).ap()


class Ctx:
    def __init__(self, nc):
        self.nc = nc
        self.st = contextlib.ExitStack()
        self.p = Prog(nc)
        self.n = 0
        self.finals = []

    UID = [0]

    def sb(self, shape, dt, name=None):
        Ctx.UID[0] += 1
        return self.st.enter_context(self.nc.sbuf_tensor("%s_%d" % (name or "sb", Ctx.UID[0]), list(shape), dt))

    def ps(self, nbanks, name=None):
        Ctx.UID[0] += 1
        return self.st.enter_context(self.nc.psum_tensor("%s_%d" % (name or "ps", Ctx.UID[0]), [128, nbanks, 512], F32))

    def finish(self):
        self.p.emit(final_wait_ops=self.finals)
        self.st.close()


class TP:
    def __init__(self, p, tag):
        self.p = p
        self.tag = tag

    def _k(self, k):
        if isinstance(k, tuple) and len(k) and isinstance(k[0], str) and k[0].startswith("@"):
            return k
        return (self.tag, k)

    def op(self, eng, fn, reads=(), writes=(), dma_slot=None, dma_n=1):
        return self.p.op(eng, fn, reads=[self._k(k) for k in reads], writes=[self._k(k) for k in writes],
                         dma_slot=(None if dma_slot is None else self._k(dma_slot)), dma_n=dma_n)


class WStream:
    def __init__(self, c, name, kc, cw, nbuf, p=None):
        self.c = c
        self.p = p or c.p
        self.name = name
        self.kc = kc
        self.nbuf = nbuf
        self.bufs = [c.sb([128, kc, cw], BF16, name="%s_%d" % (name, i)) for i in range(nbuf)]
        self.i = 0

    def load(self, w2d, kc=None, cw=None):
        b = self.i % self.nbuf
        first = (self.i == 0)
        self.i += 1
        buf = self.bufs[b]
        kc = kc or self.kc
        ncol = w2d.shape[1]
        ncc = ncol // 128
        keys = [(self.name, b, cc) for cc in range(ncc)]
        src = w2d.rearrange("(kc p) n -> p kc n", p=128)
        if first and ncc > 1:
            for cc in range(ncc):
                self.p.op("pool", lambda e, cc=cc: e.dma_start(out=buf[:, 0:kc, cc * 128:(cc + 1) * 128], in_=src[:, :, cc * 128:(cc + 1) * 128]),
                            writes=[keys[cc]], dma_slot=(self.name, b, cc, "s"))
        else:
            self.p.op("pool", lambda e: e.dma_start(out=buf[:, 0:kc, 0:ncol], in_=src), writes=keys, dma_slot=(self.name, b))
        return buf, keys


def stream(items, loader, depth):
    items = list(items)
    q = []
    nxt = 0
    for i in range(len(items)):
        while nxt < len(items) and nxt < i + depth:
            q.append(loader(items[nxt]))
            nxt += 1
        yield items[i], q.pop(0)


def rms_stats(c, src, ncols, ps_bank, ones, epsb, scr, rstd, keys_r, key_w, kc=KC, inv_d=1.0 / D, p=None, pskey="pss_bank"):
    p = p or c.p
    p.op("act", lambda e: e.activation(out=scr[:, 0:kc, 0:ncols], in_=src, func=AF.Square),
         reads=keys_r, writes=[("scr", id(scr))])

    def mm(e):
        r = None
        for k in range(kc):
            r = e.matmul(ps_bank[:, 0:ncols], lhsT=ones[:], rhs=scr[:, k, 0:ncols], start=(k == 0), stop=(k == kc - 1))
        return r
    p.op("pe", mm, reads=[("scr", id(scr))], writes=[pskey])
    p.op("act", lambda e: e.activation(out=rstd, in_=ps_bank[:, 0:ncols], func=AF.Ln, bias=epsb[:], scale=inv_d),
         reads=[pskey, "epsb"], writes=[key_w])
    p.op("act", lambda e: e.activation(out=rstd, in_=rstd, func=AF.Exp, scale=-0.5), reads=[key_w], writes=[key_w])


def consts(c, p=None):
    nc = c.nc
    p = p or c.p
    ones = c.sb([128, 128], BF16, "ones")
    epsb = c.sb([128, 1], F32, "epsb")
    p.op("dve", lambda e: e.memset(ones[:], 1.0), writes=["ones"])
    p.op("dve", lambda e: e.memset(epsb[:], EPS), writes=["epsb"])
    return ones, epsb


def phase_proj(nc, xT, g_pre, w_in, gvn, hT, qT, kT, v, uT, vn, qxT, xT_alt=None, sel=None, xownT=None, groups=(0, 1)):
    c = Ctx(nc)
    p = c.p
    ones, epsb = consts(c)
    xt = c.sb([128, KC, T], F32, "xt")
    ht = c.sb([128, KC, T], BF16, "ht")
    gsb = c.sb([128, KC], F32, "gsb")
    gvb = c.sb([128, 1024], F32, "gvb")
    scr = c.sb([128, KC, 512], BF16, "scr")
    rstd = c.sb([128, 2, 512], F32, "rstd")
    gv = c.sb([128, 8, 1024], F32, "gv")
    ssv = c.sb([128, 8, 2], F32, "ssv")
    rsv = c.sb([128, 8], F32, "rsv")
    junk = c.sb([128, 512], BF16, "junk")
    stg = [c.sb([128, T], BF16, "stg%d" % i) for i in range(3)]
    stv = [c.sb([128, 1024], BF16, "stv%d" % i) for i in range(2)]
    ps = c.ps(6, "psg")
    pss = c.ps(1, "pss")
    ws = WStream(c, "w", KC, 512, 2)

    for g in range(2):
        p.op("sp", lambda e, g=g: e.dma_start(out=xt[:, :, g * 512:(g + 1) * 512],
                                             in_=xT.rearrange("(kc p) t -> p kc t", p=128)[:, :, g * 512:(g + 1) * 512]),
             writes=[("xt", g)], dma_slot=("xt", g))
    p.op("sp", lambda e: e.dma_start(out=gsb[:], in_=g_pre), writes=["gsb"], dma_slot="gsb")
    p.op("sp", lambda e: e.dma_start(out=gvb[:], in_=gvn.partition_broadcast(128)), writes=["gvb"], dma_slot="gvb")

    if xT_alt is not None:
        selsb = c.sb([128, 2], F32, "selsb")
        p.op("sp", lambda e: e.dma_start(out=selsb[:], in_=sel), writes=["selsb"], dma_slot="selsb")
        gvkeys = [("gv", tt, hf) for tt in range(8) for hf in range(2)]
        for rnd in range(2):
            p.op("sp", lambda e, rnd=rnd: e.dma_start(out=gv[:], in_=xT_alt[rnd * 1024:(rnd + 1) * 1024, :].rearrange("(kc p) t -> p kc t", p=128)),
                 writes=gvkeys, dma_slot=("alt", rnd))
            for kk in range(8):
                k = rnd * 8 + kk
                p.op("dve", lambda e, k=k: e.tensor_scalar(out=xt[:, k, :], in0=xt[:, k, :], scalar1=selsb[:, 0:1], scalar2=None, op0=ALU.mult),
                     reads=[("xt", 0), ("xt", 1), "selsb"], writes=[("xt", 0), ("xt", 1)])
                p.op("dve", lambda e, k=k, kk=kk: e.scalar_tensor_tensor(out=xt[:, k, :], in0=gv[:, kk, :], scalar=selsb[:, 1:2], in1=xt[:, k, :],
                                                                        op0=ALU.mult, op1=ALU.add),
                     reads=[("xt", 0), ("xt", 1), "selsb"] + gvkeys, writes=[("xt", 0), ("xt", 1)])
        o = p.op("sp", lambda e: e.dma_start(out=xownT.rearrange("(kc p) t -> p kc t", p=128), in_=xt[:]), reads=[("xt", 0), ("xt", 1)], dma_slot="xown")
        c.finals.append(o)
    for g in range(2):
        rms_stats(c, xt[:, :, g * 512:(g + 1) * 512], 512, pss[:, 0, :], ones, epsb, scr, rstd[:, g, :],
                  [("xt", g), "ones", "epsb"], ("rstd", g))
        for k in range(KC):
            p.op("dve", lambda e, k=k, g=g: e.scalar_tensor_tensor(
                out=ht[:, k, g * 512:(g + 1) * 512], in0=xt[:, k, g * 512:(g + 1) * 512], scalar=gsb[:, k:k + 1],
                in1=rstd[:, g, :], op0=ALU.mult, op1=ALU.mult),
                reads=[("xt", g), "gsb", ("rstd", g)], writes=[("ht", g, k)])
    htk = [("ht", g, k) for g in range(2) for k in range(KC)]
    o = p.op("sp", lambda e: e.dma_start(out=hT.rearrange("(kc p) t -> p kc t", p=128), in_=ht[:]), reads=htk, dma_slot="hTout")
    c.finals.append(o)

    nb = [0]
    nst = [0]
    nsv = [0]

    def feat_major(buf, wkey, dst, row0, func, grp=(0, 1)):
        for cc in range(4):
            sb_ = stg[nst[0] % 3]
            skey = ("stg", nst[0] % 3)
            nst[0] += 1
            for g in grp:
                b = nb[0] % 6
                nb[0] += 1

                def mm(e, b=b, cc=cc, g=g):
                    r = None
                    for k in range(KC):
                        r = e.matmul(ps[:, b, :], lhsT=buf[:, k, cc * 128:(cc + 1) * 128], rhs=ht[:, k, g * 512:(g + 1) * 512],
                                     start=(k == 0), stop=(k == KC - 1))
                    return r
                p.op("pe", mm, reads=[wkey[cc]] + [("ht", g, k) for k in range(KC)], writes=[("ps", b)])
                p.op("act", lambda e, b=b, g=g, sb_=sb_: e.activation(out=sb_[:, g * 512:(g + 1) * 512], in_=ps[:, b, :], func=func),
                     reads=[("ps", b)], writes=[skey + (g,)])
            r0 = row0 + cc * 128
            o = p.op("sp", lambda e, sb_=sb_, r0=r0: e.dma_start(out=dst[r0:r0 + 128, :], in_=sb_[:]),
                     reads=[skey + (g,) for g in grp], dma_slot=skey)
            c.finals.append(o)

    for (s, (buf, wkey)) in stream(range(12), lambda s: ws.load(w_in[:, s * 512:(s + 1) * 512]), 2):
        sec, half = divmod(s, 2)
        if sec == 0:
            feat_major(buf, wkey, qT, half * 512, AF.Copy, groups)
        elif sec == 1:
            feat_major(buf, wkey, kT, half * 512, AF.Copy)
        elif sec == 3:
            feat_major(buf, wkey, uT, half * 512, AF.Gelu_apprx_tanh, groups)
        elif sec == 5:
            feat_major(buf, wkey, qxT, half * 512, AF.Copy, groups)
        else:
            tts = range(8) if sec == 2 else [tt for tt in range(8) if (tt // 4) in groups]
            for tt in tts:
                b = nb[0] % 6
                nb[0] += 1

                def mm(e, b=b, tt=tt, buf=buf):
                    r = None
                    for k in range(KC):
                        r = e.matmul(ps[:, b, :], lhsT=ht[:, k, tt * 128:(tt + 1) * 128], rhs=buf[:, k, :],
                                     start=(k == 0), stop=(k == KC - 1))
                    return r
                p.op("pe", mm, reads=list(wkey) + [("ht", tt // 4, k) for k in range(KC)], writes=[("ps", b)])
                if sec == 2:
                    sv = stv[nsv[0] % 2]
                    svk = ("stv", nsv[0] % 2)
                    nsv[0] += 1
                    p.op("act", lambda e, b=b, sv=sv: e.activation(out=sv[:, 0:512], in_=ps[:, b, :], func=AF.Copy),
                         reads=[("ps", b)], writes=[svk])
                    o = p.op("sp", lambda e, sv=sv, tt=tt, half=half: e.dma_start(
                        out=v[tt * 128:(tt + 1) * 128, half * 512:(half + 1) * 512], in_=sv[:, 0:512]),
                        reads=[svk], dma_slot=svk)
                    c.finals.append(o)
                else:
                    p.op("act", lambda e, b=b, tt=tt, half=half: e.activation(
                        out=gv[:, tt, half * 512:(half + 1) * 512], in_=ps[:, b, :], func=AF.Gelu_apprx_tanh),
                        reads=[("ps", b)], writes=[("gv", tt, half)])
                    p.op("act", lambda e, tt=tt, half=half: e.activation(
                        out=junk[:], in_=gv[:, tt, half * 512:(half + 1) * 512], func=AF.Square,
                        accum_out=ssv[:, tt, half:half + 1]),
                        reads=[("gv", tt, half)], writes=[("ssv", tt, half), "junk"])
    p.op("dve", lambda e: e.tensor_tensor(out=rsv[:], in0=ssv[:, :, 0], in1=ssv[:, :, 1], op=ALU.add),
         reads=[("ssv", tt, h_) for tt in range(8) for h_ in range(2)], writes=["rsv"])
    p.op("act", lambda e: e.activation(out=rsv[:], in_=rsv[:], func=AF.Ln, bias=epsb[:], scale=1.0 / 1024), reads=["rsv", "epsb"], writes=["rsv"])
    p.op("act", lambda e: e.activation(out=rsv[:], in_=rsv[:], func=AF.Exp, scale=-0.5), reads=["rsv"], writes=["rsv"])
    for tt in [tt for tt in range(8) if (tt // 4) in groups]:
        sv = stv[nsv[0] % 2]
        svk = ("stv", nsv[0] % 2)
        nsv[0] += 1
        p.op("dve", lambda e, tt=tt, sv=sv: e.scalar_tensor_tensor(
            out=sv[:], in0=gv[:, tt, :], scalar=rsv[:, tt:tt + 1], in1=gvb[:], op0=ALU.mult, op1=ALU.mult),
            reads=[("gv", tt, 0), ("gv", tt, 1), "rsv", "gvb"], writes=[svk])
        o = p.op("sp", lambda e, sv=sv, tt=tt: e.dma_start(out=vn[tt * 128:(tt + 1) * 128, :], in_=sv[:]), reads=[svk], dma_slot=svk)
        c.finals.append(o)
    c.finish()


def phase_attn2(nc, qT, kT_own, kT_prev, v_own, v_prev, prevmask, osbT, has_prev=True, side=(), groups=(0, 1)):
    c = Ctx(nc)
    p = c.p
    scale = 128 ** -0.5
    ident = c.sb([128, 128], BF16, "ident")
    negU = c.sb([128, 128], BF16, "negU")
    negO = c.sb([128, 128], BF16, "negO")
    negm = c.sb([128, 4, 512], BF16, "negm")
    pmask = c.sb([128, 512], BF16, "pmask")
    one1 = c.sb([128, 1], F32, "one1")
    kt = [c.sb([128, 2 * T], BF16, "kt%d" % i) for i in range(2)]
    vh = [c.sb([128, 16, 128], BF16, "vh%d" % i) for i in range(2)]
    qh = [c.sb([128, T], BF16, "qh%d" % i) for i in range(2)]
    eb = [c.sb([128, 1024], F32, "eb%d" % i) for i in range(2)]
    spb = [c.sb([128, 1024], BF16, "spb%d" % i) for i in range(4)]
    Gb = [c.sb([128, 1024], BF16, "Gb%d" % i) for i in range(3)]
    ab = [c.sb([128, 1024], BF16, "ab%d" % i) for i in range(3)]
    ost = [c.sb([128, T], BF16, "ost%d" % i) for i in range(2)]
    ps1 = c.ps(2, "ps1")
    ps2 = c.ps(2, "ps2")
    pso = c.ps(2, "pso")

    p.op("pool", lambda e: e.memset(ident[:], 1.0), writes=["ident"])
    p.op("pool", lambda e: e.affine_select(out=ident[:], in_=ident[:], pattern=[[-1, 128]], compare_op=ALU.is_equal,
                                           fill=0.0, base=0, channel_multiplier=1), reads=["ident"], writes=["ident"])
    p.op("pool", lambda e: e.memset(negU[:], -SQ128), writes=["negU"])
    p.op("pool", lambda e: e.affine_select(out=negU[:], in_=negU[:], pattern=[[-1, 128]], compare_op=ALU.is_ge,
                                           fill=0.0, base=0, channel_multiplier=1), reads=["negU"], writes=["negU"])
    p.op("pool", lambda e: e.memset(negO[:], -SQ128), writes=["negO"])
    p.op("pool", lambda e: e.memset(negm[:], 0.0), writes=["negm"])
    for r in range(4):
        p.op("pool", lambda e, r=r: e.affine_select(out=negm[:, r, :], in_=negm[:, r, :], pattern=[[1, 512]],
                                                    compare_op=ALU.is_gt, fill=NEG, base=-128 * r, channel_multiplier=-1),
             reads=["negm"], writes=["negm"])
    p.op("pool", lambda e: e.memset(one1[:], 1.0), writes=["one1"])
    if prevmask is not None:
        p.op("sp", lambda e: e.dma_start(out=pmask[:], in_=prevmask), writes=["pmask"], dma_slot="pmask")
    kb_lo = 0 if has_prev else 8

    def load_head(h):
        i = h % 2
        def ld(e):
            r = [e.dma_start(out=kt[i][:, T:2 * T], in_=kT_own[h * 128:(h + 1) * 128, :]),
                 e.dma_start(out=vh[i][:, 8:16, :], in_=v_own[:, h * 128:(h + 1) * 128].rearrange("(b p) d -> p b d", p=128)),
                 e.dma_start(out=qh[i][:], in_=qT[h * 128:(h + 1) * 128, :])]
            if has_prev:
                r += [e.dma_start(out=kt[i][:, 0:T], in_=kT_prev[h * 128:(h + 1) * 128, :]),
                      e.dma_start(out=vh[i][:, 0:8, :], in_=v_prev[:, h * 128:(h + 1) * 128].rearrange("(b p) d -> p b d", p=128))]
            return r
        p.op("sp", ld, writes=[("hd", i)], dma_slot=("hd", i), dma_n=(5 if has_prev else 3))
        return i

    ps1f = ps1[:].rearrange("p b n -> p (b n)")
    ps2f = ps2[:].rearrange("p b n -> p (b n)")
    psof = pso[:].rearrange("p b n -> p (b n)")
    steps = []
    for h in range(8):
        hsteps = []
        if 1 in groups:
            for kb in range(15, 11, -1):
                hsteps.append(dict(h=h, kb=kb, halves=[1], hfirst=False))
        for kb in range(11, kb_lo - 1, -1):
            hsteps.append(dict(h=h, kb=kb, halves=[hf for hf in (0, 1) if hf in groups], hfirst=False))
        hsteps[0]["hfirst"] = True
        steps += hsteps
    NSP = 4
    st = dict(G=None, gi=None, valid=[False, False], nG=0)
    head_slot = {}

    def zops(t, idx):
        h, kb, halves = t["h"], t["kb"], t["halves"]
        i = head_slot[h]
        hk = ("hd", i)
        ksl = kt[i][:, kb * 128:(kb + 1) * 128]
        c0 = 512 * halves[0]
        W = 512 * len(halves)
        t.update(i=i, hk=hk, c0=c0, W=W)
        mm = {}
        for hf in halves:
            m = [(ksl, qh[i][:, hf * 512:(hf + 1) * 512])]
            r = kb - (8 + 4 * hf)
            if r >= 0:
                m.append((ident[:], negm[:, r, :]))
            if kb < 8 and prevmask is not None:
                m.append((ident[:], pmask[:]))
            mm[hf] = m
        t["mm"] = mm
        ei = idx % 2
        si = idx % NSP
        t["si"] = si

        def zmm(e, mm=mm, halves=halves):
            rr = None
            for hf in halves:
                for j, (l_, r_) in enumerate(mm[hf]):
                    rr = e.matmul(ps1[:, hf, :], lhsT=l_, rhs=r_, start=(j == 0), stop=(j == len(mm[hf]) - 1))
            return rr
        p.op("pe", zmm, reads=[hk, "ident", "negm", "pmask"], writes=[("ps1", hf) for hf in halves])
        p.op("act", lambda e: e.activation(out=eb[ei][:, c0:c0 + W], in_=ps1f[:, c0:c0 + W], func=AF.Exp, scale=scale),
             reads=[("ps1", hf) for hf in halves], writes=[("eb", ei, hf) for hf in halves])
        p.op("act", lambda e: e.activation(out=spb[si][:, c0:c0 + W], in_=eb[ei][:, c0:c0 + W], func=AF.Ln, bias=one1[:], scale=1.0),
             reads=[("eb", ei, hf) for hf in halves] + ["one1"], writes=[("spb", si, hf) for hf in halves])
        if t["hfirst"]:
            st["valid"] = [False, False]
            st["G"] = None
        t["G"] = st["G"]
        t["Gi"] = st["gi"]
        t["Gvalid"] = list(st["valid"])
        if kb > kb_lo:
            gi = st["nG"] % 3
            st["nG"] += 1
            Gn = Gb[gi]
            Gc = st["G"]
            gci = st["gi"]
            vh_ = [hf for hf in halves if st["valid"][hf]]
            nv = [hf for hf in halves if not st["valid"][hf]]
            if len(vh_) == 2:
                p.op("dve", lambda e: e.tensor_tensor(out=Gn[:], in0=Gc[:], in1=spb[si][:], op=ALU.add),
                     reads=[("Gb", gci, 0), ("Gb", gci, 1), ("spb", si, 0), ("spb", si, 1)], writes=[("Gb", gi, 0), ("Gb", gi, 1)])
            else:
                for hf in vh_:
                    sl = slice(hf * 512, (hf + 1) * 512)
                    p.op("dve", lambda e, sl=sl: e.tensor_tensor(out=Gn[:, sl], in0=Gc[:, sl], in1=spb[si][:, sl], op=ALU.add),
                         reads=[("Gb", gci, hf), ("spb", si, hf)], writes=[("Gb", gi, hf)])
            for hf in nv:
                sl = slice(hf * 512, (hf + 1) * 512)
                p.op("dve", lambda e, sl=sl: e.tensor_copy(out=Gn[:, sl], in_=spb[si][:, sl]),
                     reads=[("spb", si, hf)], writes=[("Gb", gi, hf)])
                st["valid"][hf] = True
            st["G"] = Gn
            st["gi"] = gi

    def ps2ops(t, idx):
        halves, si, c0, W = t["halves"], t["si"], t["c0"], t["W"]
        ai = idx % 3
        t["ai"] = ai
        mm = {}
        rd = [t["hk"], "ident", "negm", "pmask", "negU", "negO"]
        for hf in halves:
            sl = slice(hf * 512, (hf + 1) * 512)
            m = list(t["mm"][hf]) + [(negU[:], spb[si][:, sl])]
            rd.append(("spb", si, hf))
            if t["Gvalid"][hf]:
                m.append((negO[:], t["G"][:, sl]))
                rd.append(("Gb", t["Gi"], hf))
            mm[hf] = m

        def mm2(e, mm=mm, halves=halves):
            rr = None
            for hf in halves:
                for j, (l_, r_) in enumerate(mm[hf]):
                    rr = e.matmul(ps2[:, hf, :], lhsT=l_, rhs=r_, start=(j == 0), stop=(j == len(mm[hf]) - 1))
            return rr
        p.op("pe", mm2, reads=rd, writes=[("ps2", hf) for hf in halves])
        p.op("act", lambda e: e.activation(out=ab[ai][:, c0:c0 + W], in_=ps2f[:, c0:c0 + W], func=AF.Exp, scale=scale),
             reads=[("ps2", hf) for hf in halves], writes=[("ab", ai, hf) for hf in halves])

    def avops(t, idx):
        h, kb, i, ai, halves = t["h"], t["kb"], t["i"], t["ai"], t["halves"]

        def av(e):
            rr = None
            for hf in halves:
                rr = e.matmul(pso[:, hf, :], lhsT=vh[i][:, kb, :], rhs=ab[ai][:, hf * 512:(hf + 1) * 512],
                              start=(kb == 11 + 4 * hf), stop=(kb == kb_lo))
            return rr
        p.op("pe", av, reads=[t["hk"]] + [("ab", ai, hf) for hf in halves], writes=[("pso", hf) for hf in halves])
        if kb == kb_lo:
            oi = h % 2
            p.op("dve", lambda e: e.tensor_copy(out=ost[oi][:], in_=psof[:, 0:1024]),
                 reads=[("pso", hf) for hf in groups], writes=[("ost", oi)])
            o = p.op("sp", lambda e: e.dma_start(out=osbT[h * 128:(h + 1) * 128, :], in_=ost[oi][:]),
                     reads=[("ost", oi)], dma_slot=("ost", oi))
            c.finals.append(o)

    n = len(steps)
    head_slot[0] = load_head(0)
    head_slot[1] = load_head(1)
    gens = []
    if side:
        psx = PsRing(c.ps(2, "psx"), 2)
        gens = [mk(c, psx) for mk in side]
    nper = 1

    def side_step():
        while gens:
            try:
                next(gens[0])
                return
            except StopIteration:
                gens.pop(0)
    for step in range(n + 2):
        for _ in range(2 if (n < 100 and step >= n - 32) else nper):
            side_step()
        if step < n:
            zops(steps[step], step)
        if 0 <= step - 1 < n:
            ps2ops(steps[step - 1], step - 1)
        if 0 <= step - 2 < n:
            t = steps[step - 2]
            avops(t, step - 2)
            if t["kb"] == kb_lo and t["h"] + 2 < 8:
                head_slot[t["h"] + 2] = load_head(t["h"] + 2)
    while gens:
        side_step()
    c.finish()


def phase_attn(nc, qT, kT_own, kT_prev, v_own, v_prev, prevmask, osbT, has_prev=True, side=()):
    c = Ctx(nc)
    p = c.p
    scale = 128 ** -0.5
    ident = c.sb([128, 128], BF16, "ident")
    negU = c.sb([128, 128], BF16, "negU")
    negO = c.sb([128, 128], BF16, "negO")
    negm = c.sb([128, 4, 512], BF16, "negm")
    pmask = c.sb([128, 512], BF16, "pmask")
    one1 = c.sb([128, 1], F32, "one1")
    kt = [c.sb([128, 2 * T], BF16, "kt%d" % i) for i in range(2)]
    vh = [c.sb([128, 16, 128], BF16, "vh%d" % i) for i in range(2)]
    qh = [c.sb([128, T], BF16, "qh%d" % i) for i in range(2)]
    eb = [c.sb([128, 512], F32, "eb%d" % i) for i in range(2)]
    spb = [c.sb([128, 512], BF16, "spb%d" % i) for i in range(4)]
    Gb = [c.sb([128, 512], BF16, "Gb%d" % i) for i in range(3)]
    ab = [c.sb([128, 512], BF16, "ab%d" % i) for i in range(3)]
    ost = [c.sb([128, T], BF16, "ost%d" % i) for i in range(2)]
    ps1 = c.ps(2, "ps1")
    ps2 = c.ps(2, "ps2")
    pso = c.ps(2, "pso")

    p.op("pool", lambda e: e.memset(ident[:], 1.0), writes=["ident"])
    p.op("pool", lambda e: e.affine_select(out=ident[:], in_=ident[:], pattern=[[-1, 128]], compare_op=ALU.is_equal,
                                           fill=0.0, base=0, channel_multiplier=1), reads=["ident"], writes=["ident"])
    p.op("pool", lambda e: e.memset(negU[:], -SQ128), writes=["negU"])
    p.op("pool", lambda e: e.affine_select(out=negU[:], in_=negU[:], pattern=[[-1, 128]], compare_op=ALU.is_ge,
                                           fill=0.0, base=0, channel_multiplier=1), reads=["negU"], writes=["negU"])
    p.op("pool", lambda e: e.memset(negO[:], -SQ128), writes=["negO"])
    p.op("pool", lambda e: e.memset(negm[:], 0.0), writes=["negm"])
    for r in range(4):
        p.op("pool", lambda e, r=r: e.affine_select(out=negm[:, r, :], in_=negm[:, r, :], pattern=[[1, 512]],
                                                    compare_op=ALU.is_gt, fill=NEG, base=-128 * r, channel_multiplier=-1),
             reads=["negm"], writes=["negm"])
    p.op("pool", lambda e: e.memset(one1[:], 1.0), writes=["one1"])
    if prevmask is not None:
        p.op("sp", lambda e: e.dma_start(out=pmask[:], in_=prevmask), writes=["pmask"], dma_slot="pmask")
    kb_lo = 0 if has_prev else 8

    def load_head(h):
        i = h % 2
        def ld(e):
            r = [e.dma_start(out=kt[i][:, T:2 * T], in_=kT_own[h * 128:(h + 1) * 128, :]),
                 e.dma_start(out=vh[i][:, 8:16, :], in_=v_own[:, h * 128:(h + 1) * 128].rearrange("(b p) d -> p b d", p=128)),
                 e.dma_start(out=qh[i][:], in_=qT[h * 128:(h + 1) * 128, :])]
            if has_prev:
                r += [e.dma_start(out=kt[i][:, 0:T], in_=kT_prev[h * 128:(h + 1) * 128, :]),
                      e.dma_start(out=vh[i][:, 0:8, :], in_=v_prev[:, h * 128:(h + 1) * 128].rearrange("(b p) d -> p b d", p=128))]
            return r
        p.op("sp", ld, writes=[("hd", i)], dma_slot=("hd", i), dma_n=(5 if has_prev else 3))
        return i

    tiles = []
    for h in range(8):
        for gq in range(2):
            top = 8 + 4 * gq + 3
            for kb in range(top, kb_lo - 1, -1):
                tiles.append(dict(h=h, gq=gq, kb=kb, top=top, first=(kb == top), last=(kb == kb_lo)))
    NSP = 4
    state = dict(loaded=-1, G=None, gkey=None, nG=0)
    head_slot = {}

    def ensure_head(h):
        assert h in head_slot, h

    def zops(t, idx):
        h, gq, kb = t["h"], t["gq"], t["kb"]
        ensure_head(h)
        i = head_slot[h]
        hk = ("hd", i)
        r = kb - (8 + 4 * gq)
        ksl = kt[i][:, kb * 128:(kb + 1) * 128]
        qsl = qh[i][:, gq * 512:(gq + 1) * 512]
        mms = [(ksl, qsl)]
        if r >= 0:
            mms.append((ident[:], negm[:, r, :]))
        if kb < 8 and prevmask is not None:
            mms.append((ident[:], pmask[:]))
        t["mms"] = mms
        t["hk"] = hk
        t["i"] = i
        b1 = idx % 2
        ei = idx % 2
        si = idx % NSP
        t["si"] = si

        def zmm(e, mms=mms, b1=b1):
            rr = None
            for j, (l_, r_) in enumerate(mms):
                rr = e.matmul(ps1[:, b1, :], lhsT=l_, rhs=r_, start=(j == 0), stop=(j == len(mms) - 1))
            return rr
        p.op("pe", zmm, reads=[hk, "ident", "negm", "pmask"], writes=[("ps1", b1)])
        p.op("act", lambda e, b1=b1, ei=ei: e.activation(out=eb[ei][:], in_=ps1[:, b1, :], func=AF.Exp, scale=scale),
             reads=[("ps1", b1)], writes=[("eb", ei)])
        p.op("act", lambda e, ei=ei, si=si: e.activation(out=spb[si][:], in_=eb[ei][:], func=AF.Ln, bias=one1[:], scale=1.0),
             reads=[("eb", ei), "one1"], writes=[("spb", si)])
        if t["first"]:
            state["G"] = None
            state["gkey"] = None
        t["G"] = state["G"]
        t["gkey"] = state["gkey"]
        if not t["last"]:
            if state["G"] is None:
                state["G"] = spb[si]
                state["gkey"] = ("spb", si)
            else:
                gi = state["nG"] % 3
                state["nG"] += 1
                Gn = Gb[gi]
                Gc = state["G"]
                p.op("dve", lambda e, Gn=Gn, Gc=Gc, si=si: e.tensor_tensor(out=Gn[:], in0=Gc[:], in1=spb[si][:], op=ALU.add),
                     reads=[state["gkey"], ("spb", si)], writes=[("Gb", gi)])
                state["G"] = Gn
                state["gkey"] = ("Gb", gi)

    def ps2ops(t, idx):
        b2 = idx % 2
        ai = idx % 3
        t["ai"] = ai
        si = t["si"]
        mms = list(t["mms"]) + [(negU[:], spb[si][:])]
        rd = [t["hk"], "ident", "negm", "pmask", "negU", "negO", ("spb", si)]
        if t["G"] is not None:
            mms.append((negO[:], t["G"][:]))
            rd.append(t["gkey"])

        def mm2(e, mms=mms, b2=b2):
            rr = None
            for j, (l_, r_) in enumerate(mms):
                rr = e.matmul(ps2[:, b2, :], lhsT=l_, rhs=r_, start=(j == 0), stop=(j == len(mms) - 1))
            return rr
        p.op("pe", mm2, reads=rd, writes=[("ps2", b2)])
        p.op("act", lambda e, b2=b2, ai=ai: e.activation(out=ab[ai][:], in_=ps2[:, b2, :], func=AF.Exp, scale=scale),
             reads=[("ps2", b2)], writes=[("ab", ai)])

    def avops(t, idx):
        h, gq, kb, i, ai = t["h"], t["gq"], t["kb"], t["i"], t["ai"]
        ob = (h * 2 + gq) % 2
        p.op("pe", lambda e: e.matmul(pso[:, ob, :], lhsT=vh[i][:, kb, :], rhs=ab[ai][:], start=t["first"], stop=t["last"]),
             reads=[t["hk"], ("ab", ai)], writes=[("pso", ob)])
        if t["last"]:
            oi = h % 2
            p.op("dve", lambda e: e.tensor_copy(out=ost[oi][:, gq * 512:(gq + 1) * 512], in_=pso[:, ob, :]),
                 reads=[("pso", ob)], writes=[("ost", oi, gq)])
            if gq == 1:
                o = p.op("sp", lambda e: e.dma_start(out=osbT[h * 128:(h + 1) * 128, :], in_=ost[oi][:]),
                         reads=[("ost", oi, 0), ("ost", oi, 1)], dma_slot=("ost", oi))
                c.finals.append(o)

    n = len(tiles)
    head_slot[0] = load_head(0)
    head_slot[1] = load_head(1)
    gens = []
    if side:
        psx = PsRing(c.ps(2, "psx"), 2)
        gens = [mk(c, psx) for mk in side]
    nside = 2 if has_prev else 1

    def side_step():
        while gens:
            try:
                next(gens[0])
                return
            except StopIteration:
                gens.pop(0)
    for step in range(n + 2):
        if step % nside == 0:
            side_step()
        if step < n:
            zops(tiles[step], step)
        if 0 <= step - 1 < n:
            ps2ops(tiles[step - 1], step - 1)
        if 0 <= step - 2 < n:
            t = tiles[step - 2]
            avops(t, step - 2)
            if t["last"] and t["gq"] == 1 and t["h"] + 2 < 8:
                head_slot[t["h"] + 2] = load_head(t["h"] + 2)
    while gens:
        side_step()
    c.finish()


def gen_gmlp(c, p, psx, uT, vn, w_sT, b_s, ogmT, groups=(0, 1)):
    wsb = c.sb([128, 8, 128], BF16, "wsb")
    bbc = c.sb([128, 8, 128], F32, "bbc")
    vsb = c.sb([128, 8, 1024], BF16, "vsb")
    usb = c.sb([128, 8, T], BF16, "usb")
    tmp = [c.sb([128, 512], F32, "tmp%d" % i) for i in range(2)]
    ost = [c.sb([128, T], BF16, "ost%d" % i) for i in range(2)]
    p.op("pool", lambda e: e.dma_start(out=wsb[:], in_=w_sT.rearrange("g s t -> s g t")), writes=["wsb"], dma_slot="wsb")
    p.op("dve", lambda e: e.memset(wsb[64:128, :, 0:64], 0.0), reads=["wsb"], writes=["wsb"])
    p.op("sp", lambda e: e.dma_start(out=bbc[:].rearrange("p g t -> p (g t)"), in_=b_s.rearrange("g t -> (g t)").partition_broadcast(128)),
         writes=["bbc"], dma_slot="bbc")
    p.op("sp", lambda e: e.dma_start(out=vsb[:], in_=vn.rearrange("(c p) f -> p c f", p=128)), writes=["vsb"], dma_slot="vsb")
    p.op("sp", lambda e: e.dma_start(out=usb[:], in_=uT.rearrange("(g p) t -> p g t", p=128)), writes=["usb"], dma_slot="usb")
    yield
    nb = 0
    for g in range(8):
        oi = g % 2
        for half in groups:
            b = psx.next()
            nb += 1

            def mm(e, b=b, g=g, half=half):
                r = None
                for cl in range(4):
                    cblk = half * 4 + cl
                    r = e.matmul(psx.t[:, b, cl * 128:(cl + 1) * 128], lhsT=vsb[:, cblk, g * 128:(g + 1) * 128], rhs=wsb[:, g, :],
                                 start=True, stop=True)
                return r
            p.op("pe", mm, reads=["vsb", "wsb"], writes=[("@psx", b)])
            ti = nb % 2
            p.op("dve", lambda e, b=b, g=g, ti=ti: e.tensor_tensor(
                out=tmp[ti][:].rearrange("p (c t) -> p c t", c=4), in0=psx.t[:, b, :].rearrange("p (c t) -> p c t", c=4),
                in1=bbc[:, g:g + 1, :].to_broadcast([128, 4, 128]), op=ALU.add),
                reads=[("@psx", b), "bbc"], writes=[("tmp", ti)])
            p.op("dve", lambda e, g=g, ti=ti, oi=oi, half=half: e.tensor_tensor(
                out=ost[oi][:, half * 512:(half + 1) * 512], in0=tmp[ti][:], in1=usb[:, g, half * 512:(half + 1) * 512], op=ALU.mult),
                reads=[("tmp", ti), "usb"], writes=[("ost", oi, half)])
            yield
        o = p.op("sp", lambda e, oi=oi, g=g: e.dma_start(out=ogmT[g * 128:(g + 1) * 128, :], in_=ost[oi][:]),
                 reads=[("ost", oi, hf) for hf in groups], dma_slot=("ost", oi))
        c.finals.append(o)


class PsRing:
    def __init__(self, t, n):
        self.t = t
        self.n = n
        self.i = 0

    def next(self):
        b = self.i % self.n
        self.i += 1
        return b


def phase_gmlp(nc, uT, vn, w_sT, b_s, ogmT):
    c = Ctx(nc)
    psx = PsRing(c.ps(4, "psx"), 4)
    for _ in gen_gmlp(c, TP(c.p, "gm"), psx, uT, vn, w_sT, b_s, ogmT):
        pass
    c.finish()


def gen_xattn(c, p, psx, memT, g_mem, w_kv, qxT, oxaT, groups=(0, 1)):
    ones, epsb = consts(c, p)
    M = 256
    mt = c.sb([128, KC, M], F32, "mt")
    mn = c.sb([128, KC, M], BF16, "mn")
    gsb = c.sb([128, KC], F32, "gsb")
    scr = c.sb([128, KC, 256], BF16, "scr")
    rstd = c.sb([128, M], F32, "rstd")
    kmT = c.sb([128, 8, M], BF16, "kmT")
    vm = c.sb([128, 2, 1024], BF16, "vm")
    qx = c.sb([128, 8, T], BF16, "qx")
    eT = [c.sb([128, 2, 512], BF16, "eT%d" % i) for i in range(2)]
    rden = [c.sb([128, 512], F32, "rden%d" % i) for i in range(2)]
    ost = [c.sb([128, T], BF16, "ost%d" % i) for i in range(2)]
    ps = psx.t
    ws = WStream(c, "w", KC, 512, 2, p=p)
    p.op("sp", lambda e: e.dma_start(out=mt[:], in_=memT.rearrange("(kc p) m -> p kc m", p=128)), writes=["mt"], dma_slot="mt")
    p.op("sp", lambda e: e.dma_start(out=gsb[:], in_=g_mem), writes=["gsb"], dma_slot="gsb")
    p.op("sp", lambda e: e.dma_start(out=qx[:], in_=qxT.rearrange("(g p) t -> p g t", p=128)), writes=["qx"], dma_slot="qx")
    b0 = psx.next()
    rms_stats(c, mt[:], M, ps[:, b0, :], ones, epsb, scr, rstd[:], ["mt", "ones", "epsb"], "rstd", p=p, pskey=("@psx", b0))
    yield
    for k in range(KC):
        p.op("dve", lambda e, k=k: e.scalar_tensor_tensor(out=mn[:, k, :], in0=mt[:, k, :], scalar=gsb[:, k:k + 1], in1=rstd[:],
                                                          op0=ALU.mult, op1=ALU.mult),
             reads=["mt", "gsb", "rstd"], writes=[("mn", k)])
    yield
    mnk = [("mn", k) for k in range(KC)]
    nb = 0
    for (s, (buf, wkey)) in stream(range(4), lambda s: ws.load(w_kv[:, s * 512:(s + 1) * 512]), 2):
        if s < 2:
            for cc in range(4):
                b = psx.next()

                def mm(e, b=b, cc=cc, buf=buf):
                    r = None
                    for k in range(KC):
                        r = e.matmul(ps[:, b, 0:M], lhsT=buf[:, k, cc * 128:(cc + 1) * 128], rhs=mn[:, k, :], start=(k == 0), stop=(k == KC - 1))
                    return r
                p.op("pe", mm, reads=[wkey[cc]] + mnk, writes=[("@psx", b)])
                p.op("act", lambda e, b=b, cc=cc, s=s: e.activation(out=kmT[:, s * 4 + cc, :], in_=ps[:, b, 0:M], func=AF.Copy),
                     reads=[("@psx", b)], writes=[("kmT", s * 4 + cc)])
                yield
        else:
            for mc in range(2):
                b = psx.next()

                def mm(e, b=b, mc=mc, buf=buf):
                    r = None
                    for k in range(KC):
                        r = e.matmul(ps[:, b, :], lhsT=mn[:, k, mc * 128:(mc + 1) * 128], rhs=buf[:, k, :], start=(k == 0), stop=(k == KC - 1))
                    return r
                p.op("pe", mm, reads=list(wkey) + mnk, writes=[("@psx", b)])
                p.op("act", lambda e, b=b, mc=mc, s=s: e.activation(out=vm[:, mc, (s - 2) * 512:(s - 1) * 512], in_=ps[:, b, :], func=AF.Copy),
                     reads=[("@psx", b)], writes=[("vm", mc, s - 2)])
                yield
    vmk = [("vm", mc, s) for mc in range(2) for s in range(2)]
    it = 0
    for hh in range(4):
        for g in groups:
            ei = it % 2
            it += 1
            for mc in range(2):
                b = psx.next()

                def mm(e, b=b, mc=mc, hh=hh, g=g):
                    r = None
                    for dc in range(2):
                        r = e.matmul(ps[:, b, :], lhsT=kmT[:, 2 * hh + dc, mc * 128:(mc + 1) * 128], rhs=qx[:, 2 * hh + dc, g * 512:(g + 1) * 512],
                                     start=(dc == 0), stop=(dc == 1))
                    return r
                p.op("pe", mm, reads=[("kmT", 2 * hh), ("kmT", 2 * hh + 1), "qx"], writes=[("@psx", b)])
                p.op("act", lambda e, b=b, mc=mc, ei=ei: e.activation(out=eT[ei][:, mc, :], in_=ps[:, b, :], func=AF.Exp, scale=1.0 / 16),
                     reads=[("@psx", b)], writes=[("eT", ei, mc)])
                yield
            di = it % 2
            bd = psx.next()

            def mmd(e, ei=ei, bd=bd):
                r = None
                for mc in range(2):
                    r = e.matmul(ps[:, bd, :], lhsT=ones[:], rhs=eT[ei][:, mc, :], start=(mc == 0), stop=(mc == 1))
                return r
            p.op("pe", mmd, reads=[("eT", ei, 0), ("eT", ei, 1), "ones"], writes=[("@psx", bd)])
            p.op("dve", lambda e, di=di, bd=bd: e.reciprocal(out=rden[di][:], in_=ps[:, bd, :]), reads=[("@psx", bd)], writes=[("rden", di)])
            yield
            for dc in range(2):
                b = psx.next()
                ch = 2 * hh + dc
                oi = ch % 2

                def mmo(e, b=b, ei=ei, ch=ch):
                    r = None
                    for mc in range(2):
                        r = e.matmul(ps[:, b, :], lhsT=vm[:, mc, ch * 128:(ch + 1) * 128], rhs=eT[ei][:, mc, :], start=(mc == 0), stop=(mc == 1))
                    return r
                p.op("pe", mmo, reads=vmk + [("eT", ei, 0), ("eT", ei, 1)], writes=[("@psx", b)])
                p.op("dve", lambda e, b=b, oi=oi, g=g, di=di: e.tensor_tensor(out=ost[oi][:, g * 512:(g + 1) * 512], in0=ps[:, b, :], in1=rden[di][:], op=ALU.mult),
                     reads=[("@psx", b), ("rden", di)], writes=[("ost", oi, g)])
                yield
        for dc in range(2):
            ch = 2 * hh + dc
            oi = ch % 2
            o = p.op("sp", lambda e, oi=oi, ch=ch: e.dma_start(out=oxaT[ch * 128:(ch + 1) * 128, :], in_=ost[oi][:]),
                     reads=[("ost", oi, g) for g in groups], dma_slot=("ost", oi))
            c.finals.append(o)


def phase_xattn(nc, memT, g_mem, w_kv, qxT, oxaT):
    c = Ctx(nc)
    psx = PsRing(c.ps(4, "psx"), 4)
    for _ in gen_xattn(c, TP(c.p, "xa"), psx, memT, g_mem, w_kv, qxT, oxaT):
        pass
    c.finish()


def phase_merge(nc, hT, obrT, w_gate, b_gate, w_br, mT, groups=(0, 1)):
    c = Ctx(nc)
    p = c.p
    ht = c.sb([128, KC, T], BF16, "ht")
    ob = [c.sb([128, 8, T], BF16, "ob%d" % i) for i in range(3)]
    bg = c.sb([128, 48], F32, "bg")
    acc = c.sb([128, 8, 512], F32, "acc")
    sig = [c.sb([128, 512], F32, "sig%d" % i) for i in range(2)]
    tmp = [c.sb([128, 512], F32, "tmp%d" % i) for i in range(2)]
    ost = [c.sb([128, T], BF16, "ost%d" % i) for i in range(2)]
    psg = c.ps(3, "psg")
    psb = c.ps(3, "psb")
    wg = WStream(c, "wg", KC, 512, 2)
    wb = WStream(c, "wb", 8, 512, 2)
    p.op("sp", lambda e: e.dma_start(out=ht[:], in_=hT.rearrange("(kc p) t -> p kc t", p=128)), writes=["ht"], dma_slot="ht")
    for i in range(3):
        p.op("act" if i % 2 == 0 else "sp", lambda e, i=i: e.dma_start(out=ob[i][:], in_=obrT[i].rearrange("(kc p) t -> p kc t", p=128)), writes=[("ob", i)], dma_slot=("ob", i))
    p.op("sp", lambda e: e.dma_start(out=bg[:], in_=b_gate), writes=["bg"], dma_slot="bg")
    items = [(s, br) for s in range(4) for br in range(3)]

    def loader(it):
        s, br = it
        return (wg.load(w_gate[:, br * 2048 + s * 512: br * 2048 + (s + 1) * 512]), wb.load(w_br[br][:, s * 512:(s + 1) * 512]))
    n = 0
    for ((s, br), ((gbuf, gkey), (bbuf, bkey))) in stream(items, loader, 2):
        for cc in range(4):
            fch = s * 4 + cc
            oi = fch % 2
            for g in groups:
                b = n % 3
                n += 1

                def mmg(e, b=b, cc=cc, g=g, gbuf=gbuf):
                    r = None
                    for k in range(KC):
                        r = e.matmul(psg[:, b, :], lhsT=gbuf[:, k, cc * 128:(cc + 1) * 128], rhs=ht[:, k, g * 512:(g + 1) * 512], start=(k == 0), stop=(k == KC - 1))
                    return r
                p.op("pe", mmg, reads=[gkey[cc], "ht"], writes=[("psg", b)])

                def mmb(e, b=b, cc=cc, g=g, bbuf=bbuf, br=br):
                    r = None
                    for k in range(8):
                        r = e.matmul(psb[:, b, :], lhsT=bbuf[:, k, cc * 128:(cc + 1) * 128], rhs=ob[br][:, k, g * 512:(g + 1) * 512], start=(k == 0), stop=(k == 7))
                    return r
                p.op("pe", mmb, reads=[bkey[cc], ("ob", br)], writes=[("psb", b)])
                si = n % 2
                col = br * 16 + fch
                p.op("act", lambda e, b=b, si=si, col=col: e.activation(out=sig[si][:], in_=psg[:, b, :], func=AF.Sigmoid, bias=bg[:, col:col + 1], scale=1.0),
                     reads=[("psg", b), "bg"], writes=[("sig", si)])
                ak = ("acc", cc, g)
                asl = acc[:, cc * 2 + g, :]
                if br == 0:
                    p.op("dve", lambda e, b=b, si=si, asl=asl: e.tensor_tensor(out=asl, in0=psb[:, b, :], in1=sig[si][:], op=ALU.mult),
                         reads=[("psb", b), ("sig", si)], writes=[ak])
                else:
                    p.op("dve", lambda e, b=b, si=si: e.tensor_tensor(out=tmp[si][:], in0=psb[:, b, :], in1=sig[si][:], op=ALU.mult),
                         reads=[("psb", b), ("sig", si)], writes=[("tmp", si)])
                    if br == 1:
                        p.op("pool", lambda e, si=si, asl=asl: e.tensor_tensor(out=asl, in0=asl, in1=tmp[si][:], op=ALU.add),
                             reads=[("tmp", si), ak], writes=[ak])
                    else:
                        p.op("pool", lambda e, si=si, asl=asl, oi=oi, g=g: e.tensor_tensor(out=ost[oi][:, g * 512:(g + 1) * 512], in0=asl, in1=tmp[si][:], op=ALU.add),
                             reads=[("tmp", si), ak], writes=[("ost", oi, g)])
            if br == 2:
                o = p.op("sp", lambda e, oi=oi, fch=fch: e.dma_start(out=mT[fch * 128:(fch + 1) * 128, :], in_=ost[oi][:]),
                         reads=[("ost", oi, g) for g in groups], dma_slot=("ost", oi))
                c.finals.append(o)
    c.finish()


def phase_proj_norm_res(nc, inT, kc_in, w, g_post, resT, outT, cw, res_cols0=0, groups=(0, 1)):
    c = Ctx(nc)
    p = c.p
    ones, epsb = consts(c)
    it = c.sb([128, kc_in, T], BF16, "it")
    y = c.sb([128, KC, T], F32, "y")
    sq = [c.sb([128, 512], BF16, "sq%d" % i) for i in range(3)]
    gsb = c.sb([128, KC], F32, "gsb")
    rstd = c.sb([128, 2, 512], F32, "rstd")
    nwb = 4 if kc_in <= 16 else 3
    nxr = 4 if kc_in <= 16 else 3
    xr = [c.sb([128, T], F32, "xr%d" % i) for i in range(nxr)]
    ps = c.ps(4, "ps")
    pss = c.ps(2, "pss")
    ws = WStream(c, "w", kc_in, cw, nwb)
    NPC = 4
    bnd = [kc_in * i // NPC for i in range(NPC + 1)]
    for i in range(NPC):
        p.op("sp" if i % 2 == 0 else "act", lambda e, i=i: e.dma_start(out=it[:, bnd[i]:bnd[i + 1], :], in_=inT[bnd[i] * 128:bnd[i + 1] * 128, :].rearrange("(kc p) t -> p kc t", p=128)),
             writes=[("it", i)], dma_slot=("it", i))
    p.op("sp", lambda e: e.dma_start(out=gsb[:], in_=g_post), writes=["gsb"], dma_slot="gsb")
    nslab = 2048 // cw
    cpers = cw // 128
    n = 0
    pend = []
    for (s, (buf, wkey)) in stream(range(nslab), lambda s: ws.load(w[:, s * cw:(s + 1) * cw]), nwb):
        for cc in range(cpers):
            fch = s * cpers + cc
            for g in groups:
                b = n % 4
                n += 1

                for i in range(NPC):
                    def mm(e, b=b, cc=cc, g=g, buf=buf, i=i):
                        r = None
                        for k in range(bnd[i], bnd[i + 1]):
                            r = e.matmul(ps[:, b, :], lhsT=buf[:, k, cc * 128:(cc + 1) * 128], rhs=it[:, k, g * 512:(g + 1) * 512], start=(k == 0), stop=(k == kc_in - 1))
                        return r
                    p.op("pe", mm, reads=[wkey[cc], ("it", i)], writes=[("ps", b)])
                p.op("act", lambda e, b=b, fch=fch, g=g: e.activation(out=y[:, fch, g * 512:(g + 1) * 512], in_=ps[:, b, :], func=AF.Copy),
                     reads=[("ps", b)], writes=[("y", fch, g)])
                qi = n % 3
                p.op("act", lambda e, b=b, qi=qi: e.activation(out=sq[qi][:], in_=ps[:, b, :], func=AF.Square),
                     reads=[("ps", b)], writes=[("sq", qi)])
                if pend:
                    pend.pop()()
                pend.append(lambda qi=qi, g=g, fch=fch: p.op(
                    "pe", lambda e: e.matmul(pss[:, g, :], lhsT=ones[:], rhs=sq[qi][:], start=(fch == 0), stop=(fch == KC - 1)),
                    reads=[("sq", qi), "ones"], writes=[("pss", g)]))
    while pend:
        pend.pop()()
    for g in groups:
        p.op("act", lambda e, g=g: e.activation(out=rstd[:, g, :], in_=pss[:, g, :], func=AF.Ln, bias=epsb[:], scale=1.0 / D),
             reads=[("pss", g), "epsb"], writes=[("rstd", g)])
        p.op("act", lambda e, g=g: e.activation(out=rstd[:, g, :], in_=rstd[:, g, :], func=AF.Exp, scale=-0.5), reads=[("rstd", g)], writes=[("rstd", g)])
    for fch in range(KC):
        xi = fch % nxr
        p.op("sp", lambda e, xi=xi, fch=fch: e.dma_start(out=xr[xi][:], in_=resT[fch * 128:(fch + 1) * 128, res_cols0:res_cols0 + T]),
             writes=[("xr", xi)], dma_slot=("xr", xi))
        for g in groups:
            p.op("dve", lambda e, fch=fch, g=g: e.scalar_tensor_tensor(out=y[:, fch, g * 512:(g + 1) * 512], in0=y[:, fch, g * 512:(g + 1) * 512],
                                                                      scalar=gsb[:, fch:fch + 1], in1=rstd[:, g, :], op0=ALU.mult, op1=ALU.mult),
                 reads=[("y", fch, g), "gsb", ("rstd", g)], writes=[("y", fch, g)])
            p.op("dve" if g == 0 else "pool", lambda e, fch=fch, g=g, xi=xi: e.tensor_tensor(
                out=y[:, fch, g * 512:(g + 1) * 512], in0=y[:, fch, g * 512:(g + 1) * 512], in1=xr[xi][:, g * 512:(g + 1) * 512], op=ALU.add),
                reads=[("y", fch, g), ("xr", xi)], writes=[("y", fch, g)])
        o = p.op("act", lambda e, fch=fch: e.dma_start(out=outT[fch * 128:(fch + 1) * 128, :], in_=y[:, fch, :]),
                 reads=[("y", fch, g) for g in groups], dma_slot=("yo", fch % 4))
        c.finals.append(o)
    c.finish()


def phase_ffn_up(nc, xmT_h, g_pre, w_up, conv_w, conv_b, actT, xmT=None, haloT=None, hv=None):
    c = Ctx(nc)
    p = c.p
    ones, epsb = consts(c)
    TH = T + 2
    xh = c.sb([128, KC, TH], F32, "xh")
    h2 = c.sb([128, KC, TH], BF16, "h2")
    gsb = c.sb([128, KC], F32, "gsb")
    cwsb = c.sb([128, 44, 3], F32, "cwsb")
    cbsb = c.sb([128, 44], F32, "cbsb")
    scr = c.sb([128, KC, 512], BF16, "scr")
    rstd = c.sb([128, TH], F32, "rstd")
    gs = [c.sb([128, TH], F32, "gs%d" % i) for i in range(2)]
    cv = [c.sb([128, T], F32, "cv%d" % i) for i in range(2)]
    gl = [c.sb([128, T], F32, "gl%d" % i) for i in range(2)]
    ost = [c.sb([128, T], BF16, "ost%d" % i) for i in range(2)]
    psg = c.ps(3, "psg")
    psv = c.ps(3, "psv")
    pss = c.ps(1, "pss")
    CW = 256
    wg = WStream(c, "wg", KC, CW, 2)
    wv = WStream(c, "wv", KC, CW, 2)
    if xmT_h is not None:
        p.op("sp", lambda e: e.dma_start(out=xh[:], in_=xmT_h.rearrange("(kc p) t -> p kc t", p=128)), writes=["xh"], dma_slot="xh")
    else:
        p.op("sp", lambda e: e.dma_start(out=xh[:, :, 2:TH], in_=xmT.rearrange("(kc p) t -> p kc t", p=128)), writes=["xh"], dma_slot="xh")
        if haloT is not None:
            p.op("sp", lambda e: e.dma_start(out=xh[:, :, 0:2], in_=haloT.rearrange("(kc p) t -> p kc t", p=128)), writes=["xhh"], dma_slot="xhh")
            if hv is not None:
                hvsb = c.sb([128, 1], F32, "hvsb")
                p.op("sp", lambda e: e.dma_start(out=hvsb[:], in_=hv), writes=["hvsb"], dma_slot="hvsb")
                p.op("dve", lambda e: e.tensor_scalar(out=xh[:, :, 0:2], in0=xh[:, :, 0:2], scalar1=hvsb[:, 0:1], scalar2=None, op0=ALU.mult),
                     reads=["xhh", "hvsb"], writes=["xhh"])
        else:
            p.op("dve", lambda e: e.memset(xh[:, :, 0:2], 0.0), writes=["xhh"])
    p.op("sp", lambda e: e.dma_start(out=gsb[:], in_=g_pre), writes=["gsb"], dma_slot="gsb")
    p.op("act", lambda e: e.dma_start(out=cwsb[:], in_=conv_w), writes=["cwsb"], dma_slot="cwsb")
    p.op("act", lambda e: e.dma_start(out=cbsb[:], in_=conv_b), writes=["cbsb"], dma_slot="cbsb")
    segs = [(0, 2), (2, 512), (514, 512)]
    zero_halo = (xmT_h is None and haloT is None)
    for si, (c0, n) in enumerate(segs):
        if si == 0 and zero_halo:
            continue
        rms_stats(c, xh[:, :, c0:c0 + n], n, pss[:, 0, :], ones, epsb, scr, rstd[:, c0:c0 + n], ["xh", "xhh", "ones", "epsb"], ("rstd", si))
        for k in range(KC):
            p.op("dve", lambda e, k=k, c0=c0, n=n: e.scalar_tensor_tensor(
                out=h2[:, k, c0:c0 + n], in0=xh[:, k, c0:c0 + n], scalar=gsb[:, k:k + 1], in1=rstd[:, c0:c0 + n], op0=ALU.mult, op1=ALU.mult),
                reads=["xh", "xhh", "gsb", ("rstd", si)], writes=[("h2", si, k)])
    h2k = [("h2", si, k) for si in range(1 if zero_halo else 0, 3) for k in range(KC)]
    nslab = 5632 // CW
    cpers = CW // 128

    def loader(s):
        return (wg.load(w_up[:, s * CW:(s + 1) * CW]), wv.load(w_up[:, 5632 + s * CW: 5632 + (s + 1) * CW]))
    n = 0
    for (s, ((gbuf, gkey), (vbuf, vkey))) in stream(range(nslab), loader, 2):
        for cc in range(cpers):
            j = s * cpers + cc
            ji = j % 2
            gkeys = []
            for si, (c0, nn) in enumerate(segs):
                if si == 0 and zero_halo:
                    p.op("pool", lambda e, ji=ji: e.memset(gs[ji][:, 0:2], 0.0), writes=[("gs", ji, 0)])
                    gkeys.append(("gs", ji, 0))
                    continue
                b = n % 3
                n += 1

                def mmg(e, b=b, cc=cc, c0=c0, nn=nn, gbuf=gbuf):
                    r = None
                    for k in range(KC):
                        r = e.matmul(psg[:, b, 0:nn], lhsT=gbuf[:, k, cc * 128:(cc + 1) * 128], rhs=h2[:, k, c0:c0 + nn], start=(k == 0), stop=(k == KC - 1))
                    return r
                p.op("pe", mmg, reads=[gkey[cc]] + [("h2", si, k) for k in range(KC)], writes=[("psg", b)])
                p.op("act", lambda e, b=b, ji=ji, c0=c0, nn=nn: e.activation(out=gs[ji][:, c0:c0 + nn], in_=psg[:, b, 0:nn], func=AF.Copy),
                     reads=[("psg", b)], writes=[("gs", ji, si)])
                gkeys.append(("gs", ji, si))
            p.op("dve", lambda e, ji=ji, j=j: e.tensor_scalar(out=cv[ji][:], in0=gs[ji][:, 2:2 + T], scalar1=cwsb[:, j, 2:3], scalar2=cbsb[:, j:j + 1],
                                                              op0=ALU.mult, op1=ALU.add),
                 reads=gkeys + ["cwsb", "cbsb"], writes=[("cv", ji)])
            p.op("dve", lambda e, ji=ji, j=j: e.scalar_tensor_tensor(out=cv[ji][:], in0=gs[ji][:, 1:1 + T], scalar=cwsb[:, j, 1:2], in1=cv[ji][:],
                                                                     op0=ALU.mult, op1=ALU.add),
                 reads=gkeys + ["cwsb", ("cv", ji)], writes=[("cv", ji)])
            p.op("dve", lambda e, ji=ji, j=j: e.scalar_tensor_tensor(out=cv[ji][:], in0=gs[ji][:, 0:T], scalar=cwsb[:, j, 0:1], in1=cv[ji][:],
                                                                     op0=ALU.mult, op1=ALU.add),
                 reads=gkeys + ["cwsb", ("cv", ji)], writes=[("cv", ji)])
            p.op("act", lambda e, ji=ji: e.activation(out=gl[ji][:], in_=cv[ji][:], func=AF.Gelu_apprx_tanh), reads=[("cv", ji)], writes=[("gl", ji)])
            for g in range(2):
                b = n % 3
                n += 1

                def mmv(e, b=b, cc=cc, g=g, vbuf=vbuf):
                    r = None
                    for k in range(KC):
                        r = e.matmul(psv[:, b, :], lhsT=vbuf[:, k, cc * 128:(cc + 1) * 128], rhs=h2[:, k, 2 + g * 512: 2 + (g + 1) * 512], start=(k == 0), stop=(k == KC - 1))
                    return r
                p.op("pe", mmv, reads=[vkey[cc]] + [("h2", g + 1, k) for k in range(KC)], writes=[("psv", b)])
                p.op("dve", lambda e, b=b, ji=ji, g=g: e.tensor_tensor(out=ost[ji][:, g * 512:(g + 1) * 512], in0=psv[:, b, :], in1=gl[ji][:, g * 512:(g + 1) * 512], op=ALU.mult),
                     reads=[("psv", b), ("gl", ji)], writes=[("ost", ji, g)])
            o = p.op("sp", lambda e, ji=ji, j=j: e.dma_start(out=actT[j * 128:(j + 1) * 128, :], in_=ost[ji][:]),
                     reads=[("ost", ji, 0), ("ost", ji, 1)], dma_slot=("ost", ji))
            c.finals.append(o)
    c.finish()


import ml_dtypes
from concourse.bass_utils import run_bass_kernel_spmd

_BF = ml_dtypes.bfloat16
_NC = [None]
S = 2048


def _build(L=2, dbg=False):
    nc = bass.Bass("TRN2", target_bir_lowering=False)
    I = lambda n, s, dt=F32: dram(nc, n, s, dt, "ExternalInput")
    N = lambda n, s, dt=BF16: dram(nc, n, s, dt, "ExternalOutput" if dbg else "Internal")
    xT = I("xT", [2048, S])
    memT = I("memT", [2048, 256])
    w_in = I("w_in", [L, 2048, 6144])
    w_kv = I("w_kv", [L, 2048, 2048])
    w_gate = I("w_gate", [L, 2048, 6144])
    w_br = [I("w_br%d" % i, [L, 1024, 2048]) for i in range(3)]
    w_out = I("w_out", [L, 2048, 2048])
    w_up = I("w_up", [L, 2048, 11264])
    w_down = I("w_down", [L, 5632, 2048])
    g_mix_pre = I("g_mix_pre", [L, 128, 16])
    gvn = I("gvn", [L, 1024])
    w_sT = I("w_sT", [L, 8, 128, 128])
    b_s = I("b_s", [L, 8, 128])
    g_mem = I("g_mem", [L, 128, 16])
    b_gate = I("b_gate", [L, 128, 48])
    g_mix_post = I("g_mix_post", [L, 128, 16])
    g_ffn_pre = I("g_ffn_pre", [L, 128, 16])
    conv_w = I("conv_w", [L, 128, 44, 3])
    conv_b = I("conv_b", [L, 128, 44])
    g_ffn_post = I("g_ffn_post", [L, 128, 16])
    outT = dram(nc, "outT", [2048, T], F32, "ExternalOutput")

    x1T = N("x1T", [2048, S], F32)
    xmT = [N("xmT%d" % l, [2048, S], F32) for l in range(2)]
    hT = N("hT", [2048, T])
    qT = N("qT", [1024, T])
    uT = N("uT", [1024, T])
    vn = N("vn", [T, 1024])
    qxT = N("qxT", [1024, T])
    osbT = N("osbT", [1024, T])
    ogmT = N("ogmT", [1024, T])
    oxaT = N("oxaT", [1024, T])
    mT = N("mT", [2048, T])
    actT = N("actT", [5632, T])
    kT = [[N("kT%d%d" % (l, h), [1024, T]) for h in range(2)] for l in range(2)]
    vv = [[N("v%d%d" % (l, h), [T, 1024]) for h in range(2)] for l in range(2)]

    sel = I("sel", [128, 2])
    pm = I("pm", [128, 512], BF16)
    hv = I("hv", [128, 1])
    xownT = N("xownT", [2048, T], F32)
    xmownT = N("xmownT", [2048, T], F32)
    kTo = N("kTo", [1024, T])
    vo = N("vo", [T, 1024])

    def mixer(l, xin_h, kT_own, v_own, kT_prev, v_prev, has_prev, prevmask, res, xm_out, blend=None, groups=(0, 1)):
        if blend is None:
            phase_proj(nc, xin_h, g_mix_pre[l], w_in[l], gvn[l], hT, qT, kT_own, v_own, uT, vn, qxT, groups=groups)
        else:
            phase_proj(nc, xin_h, g_mix_pre[l], w_in[l], gvn[l], hT, qT, kT_own, v_own, uT, vn, qxT, xT_alt=blend, sel=sel, xownT=xownT)
        side = [lambda c, psx: gen_xattn(c, TP(c.p, "xa"), psx, memT, g_mem[l], w_kv[l], qxT, oxaT, groups=groups),
                lambda c, psx: gen_gmlp(c, TP(c.p, "gm"), psx, uT, vn, w_sT[l], b_s[l], ogmT, groups=groups)]
        phase_attn2(nc, qT, kT_own, kT_prev, v_own, v_prev, prevmask, osbT, has_prev=has_prev, side=side, groups=groups)
        phase_merge(nc, hT, [osbT, ogmT, oxaT], w_gate[l], b_gate[l], [w_br[i][l] for i in range(3)], mT, groups=groups)
        phase_proj_norm_res(nc, mT, 16, w_out[l], g_mix_post[l], res, xm_out, 512, groups=groups)

    def ffn(l, xm_h, halo, hv_, out_h):
        phase_ffn_up(nc, None, g_ffn_pre[l], w_up[l], conv_w[l], conv_b[l], actT, xmT=xm_h, haloT=halo, hv=hv_)
        phase_proj_norm_res(nc, actT, 44, w_down[l], g_ffn_post[l], xm_h, out_h, 128)

    h0 = slice(0, T)
    h1 = slice(T, 2 * T)
    l = 0
    for h, tsl in enumerate((h0, h1)):
        mixer(l, xT[:, tsl], kT[l][h], vv[l][h], kT[l][0], vv[l][0], h == 1, None, xT[:, tsl], xmT[l][:, tsl])
        ffn(l, xmT[l][:, tsl], (xmT[l][:, T - 2:T] if h == 1 else None), None, x1T[:, tsl])
    l = 1
    mixer(l, x1T[:, h0], kT[l][0], vv[l][0], kT[l][0], vv[l][0], False, None, x1T[:, h0], xmT[l][:, h0], groups=(1,))
    mixer(l, x1T[:, h0], kTo, vo, kT[l][0], vv[l][0], True, pm, xownT, xmownT, blend=x1T[:, h1])
    ffn(l, xmownT, xmT[l][:, T - 2:T], hv, outT)
    return nc


def _pl(v, n):
    v = np.asarray(v, dtype=np.float32)
    return np.ascontiguousarray(v.reshape(v.shape[0], n, 128).transpose(0, 2, 1))


def kernel(**inp):
    inp = {k: np.asarray(v) for k, v in inp.items()}
    x = inp["x"]
    mem = inp["mem"]
    if _NC[0] is None:
        _NC[0] = _build()
    nc = _NC[0]
    f32 = lambda a: np.ascontiguousarray(np.asarray(a, dtype=np.float32))
    shared = {
        "w_in": f32(inp["w_in"]), "w_kv": f32(inp["w_mem_kv"]), "w_gate": f32(inp["w_gate"]),
        "w_br0": f32(inp["w_br_sb"]), "w_br1": f32(inp["w_br_gm"]), "w_br2": f32(inp["w_br_xa"]),
        "w_out": f32(inp["w_out"]), "w_up": f32(inp["w_up"]), "w_down": f32(inp["w_down"]),
        "g_mix_pre": _pl(inp["g_mix_pre"], 16), "gvn": f32(inp["g_vnorm"]),
        "w_sT": np.ascontiguousarray(inp["w_s"].transpose(0, 1, 3, 2)), "b_s": f32(inp["b_s"]),
        "g_mem": _pl(inp["g_mem"], 16), "b_gate": _pl(inp["b_gate"], 48), "g_mix_post": _pl(inp["g_mix_post"], 16),
        "g_ffn_pre": _pl(inp["g_ffn_pre"], 16),
        "conv_w": np.ascontiguousarray(inp["conv_w"].reshape(2, 3, 44, 128).transpose(0, 3, 2, 1)),
        "conv_b": _pl(inp["conv_b"], 44), "g_ffn_post": _pl(inp["g_ffn_post"], 16),
    }
    in_maps = []
    for c in range(8):
        b, par = divmod(c, 2)
        m = dict(shared)
        m["xT"] = np.ascontiguousarray(x[b].T)
        m["memT"] = np.ascontiguousarray(mem[b].T)
        m["sel"] = np.tile(np.array([[1.0 - par, float(par)]], np.float32), (128, 1))
        m["pm"] = np.full((128, 512), (0.0 if par == 1 else NEG), dtype=np.float32).astype(_BF)
        m["hv"] = np.full((128, 1), float(par), np.float32)
        in_maps.append(m)
    res = run_bass_kernel_spmd(nc, in_maps, core_ids=list(range(8)))
    out = np.empty_like(x)
    for c in range(8):
        b, par = divmod(c, 2)
        out[b, par * T:(par + 1) * T] = res.results[c]["outT"].T
    return out
```
